# Optimizing a Trainium2 kernel written in Bass

```python
import math
import jax, jax.numpy as jnp
from jax import lax
import numpy as np

D_MODEL = 1024
BATCH = 8
SEQ = 2048
DEPTH = 2
DEC_BATCH = 8
DEC_SEQ = 4096
PAST_LEN = 128

M_HEADS = 4
M_QK_DIM = 256
M_V_DIM = 256
M_CHUNK = 128
M_IN = 2 * M_HEADS * M_QK_DIM + 2 * M_HEADS * M_V_DIM + 4 * M_HEADS
A_HEADS = 16
A_KV_HEADS = 4
A_GROUP = A_HEADS // A_KV_HEADS
A_HEAD_DIM = 64
WINDOW = 128
A_BLOCK = 128
A_SPAN = A_BLOCK + 2 * WINDOW
A_IN = A_HEADS * A_HEAD_DIM + 2 * A_KV_HEADS * A_HEAD_DIM
NUM_BUCKETS = 32
MAX_DISTANCE = 128
D_FF = 2816
CONV_WIDTH = 3
EPS = 1e-6
N_MIXERS = 2
N_MLSTM_LAYERS = (DEPTH + 1) // 2
N_ATTN_LAYERS = DEPTH // 2

kernel_name = 'hybrid_mlstm_swa_convffn_encoder'


def rmsnorm(x, g):
    xf = x.astype(jnp.float32)
    y = xf * lax.rsqrt(jnp.mean(xf * xf, axis=-1, keepdims=True) + EPS)
    return (y * g.astype(jnp.float32)).astype(x.dtype)


def mlstm_scan(q, k, v, ig, lf):
    B, H, S, DK = q.shape
    DV = v.shape[-1]
    nc = S // M_CHUNK
    chunk = lambda t: jnp.moveaxis(t.reshape(B, H, nc, M_CHUNK, *t.shape[3:]), 2, 0)
    xs = (chunk(q), chunk(k), chunk(v), chunk(ig), chunk(lf))
    tril = jnp.tril(jnp.ones((M_CHUNK, M_CHUNK), dtype=bool))

    def step(carry, inp):
        C, n, m = carry
        qc, kc, vc, igc, lfc = inp
        b = jnp.cumsum(lfc, axis=-1)
        log_d = b[..., :, None] - b[..., None, :] + igc[..., None, :]
        log_d = jnp.where(tril, log_d, -jnp.inf)
        m_inter = b + m[..., None]
        m_t = jnp.maximum(jnp.max(log_d, axis=-1), m_inter)
        s = jnp.einsum('bhtd,bhsd->bhts', qc, kc) * jnp.exp(log_d - m_t[..., None])
        w_inter = jnp.exp(m_inter - m_t)[..., None]
        num = jnp.einsum('bhts,bhse->bhte', s, vc) + w_inter * jnp.einsum('bhtd,bhde->bhte', qc, C)
        den = jnp.sum(s, axis=-1, keepdims=True) + w_inter * jnp.einsum('bhtd,bhd->bht', qc, n)[..., None]
        h = num / jnp.maximum(jnp.abs(den), jnp.exp(-m_t)[..., None])
        b_last = b[..., -1]
        log_w = b_last[..., None] - b + igc
        m_new = jnp.maximum(b_last + m, jnp.max(log_w, axis=-1))
        decay = jnp.exp(b_last + m - m_new)
        w = jnp.exp(log_w - m_new[..., None])
        C_new = decay[..., None, None] * C + jnp.einsum('bhs,bhsd,bhse->bhde', w, kc, vc)
        n_new = decay[..., None] * n + jnp.einsum('bhs,bhsd->bhd', w, kc)
        return (C_new, n_new, m_new), h

    init = (jnp.zeros((B, H, DK, DV), jnp.float32), jnp.zeros((B, H, DK), jnp.float32),
            jnp.zeros((B, H), jnp.float32))
    _, hs = lax.scan(step, init, xs)
    return jnp.moveaxis(hs, 0, 2).reshape(B, H, S, DV)


def mlstm_mixer(h, w_in, b_gate, head_g, w_out):
    B, S, _ = h.shape
    qk = M_HEADS * M_QK_DIM
    vw = M_HEADS * M_V_DIM
    proj = h @ w_in
    q = proj[..., :qk]
    k = proj[..., qk:2 * qk]
    v = proj[..., 2 * qk:2 * qk + vw]
    o = proj[..., 2 * qk + vw:2 * qk + 2 * vw]
    gates = (proj[..., 2 * qk + 2 * vw:] + b_gate).astype(jnp.float32).reshape(B, S, 2, 2, M_HEADS)
    gates = jnp.transpose(gates, (2, 3, 0, 4, 1))
    heads = lambda t, d: jnp.transpose(t.reshape(B, S, M_HEADS, d), (0, 2, 1, 3)).astype(jnp.float32)
    qh = heads(q, M_QK_DIM) * (M_QK_DIM ** -0.5)
    kh = heads(k, M_QK_DIM)
    vh = heads(v, M_V_DIM)
    h_fwd = mlstm_scan(qh, kh, vh, gates[0, 0], jax.nn.log_sigmoid(gates[0, 1]))
    rev = lambda t: jnp.flip(t, axis=2)
    h_bwd = rev(mlstm_scan(rev(qh), rev(kh), rev(vh), rev(gates[1, 0]),
                           rev(jax.nn.log_sigmoid(gates[1, 1]))))
    hs = h_fwd + h_bwd
    hs = hs * lax.rsqrt(jnp.mean(hs * hs, axis=-1, keepdims=True) + EPS) * head_g.astype(jnp.float32)[None, :, None, :]
    hs = jnp.transpose(hs, (0, 2, 1, 3)).reshape(B, S, vw).astype(h.dtype)
    return (hs * jax.nn.sigmoid(o)) @ w_out


def t5_bucket(rel):
    nb = NUM_BUCKETS // 2
    max_exact = nb // 2
    ret = jnp.where(rel > 0, nb, 0)
    n = jnp.abs(rel)
    nf = jnp.maximum(n, 1).astype(jnp.float32)
    large = max_exact + (jnp.log(nf / max_exact) / math.log(MAX_DISTANCE / max_exact)
                         * (nb - max_exact)).astype(jnp.int32)
    large = jnp.minimum(large, nb - 1)
    return ret + jnp.where(n < max_exact, n, large)


def swa_mixer(h, w_in, sink, rel_bias, w_out):
    B, S, _ = h.shape
    qw = A_HEADS * A_HEAD_DIM
    kw = A_KV_HEADS * A_HEAD_DIM
    proj = h @ w_in
    q = proj[..., :qw].reshape(B, S, A_KV_HEADS, A_GROUP, A_HEAD_DIM)
    k = proj[..., qw:qw + kw].reshape(B, S, A_KV_HEADS, A_HEAD_DIM)
    v = proj[..., qw + kw:].reshape(B, S, A_KV_HEADS, A_HEAD_DIM)
    k_pad = jnp.pad(k, ((0, 0), (WINDOW, WINDOW), (0, 0), (0, 0)))
    v_pad = jnp.pad(v, ((0, 0), (WINDOW, WINDOW), (0, 0), (0, 0)))
    a_idx = jnp.arange(A_BLOCK)[:, None]
    c_idx = jnp.arange(A_SPAN)[None, :]
    rel = c_idx - WINDOW - a_idx
    band = jnp.abs(rel) <= WINDOW
    bias = jnp.transpose(rel_bias.astype(jnp.float32)[t5_bucket(rel)], (2, 0, 1))
    bias = bias.reshape(A_KV_HEADS, A_GROUP, A_BLOCK, A_SPAN)
    sink_l = sink.astype(jnp.float32).reshape(1, A_KV_HEADS, A_GROUP, 1, 1)
    scale = A_HEAD_DIM ** -0.5

    def block(i):
        qb = lax.dynamic_slice_in_dim(q, i * A_BLOCK, A_BLOCK, axis=1)
        kb = lax.dynamic_slice_in_dim(k_pad, i * A_BLOCK, A_SPAN, axis=1)
        vb = lax.dynamic_slice_in_dim(v_pad, i * A_BLOCK, A_SPAN, axis=1)
        s = jnp.einsum('bqkgd,bckd->bkgqc', qb, kb).astype(jnp.float32) * scale + bias
        key_pos = i * A_BLOCK - WINDOW + jnp.arange(A_SPAN)
        valid = band & ((key_pos >= 0) & (key_pos < S))[None, :]
        s = jnp.where(valid, s, -jnp.inf)
        m = jnp.maximum(jnp.max(s, axis=-1, keepdims=True), sink_l)
        p = jnp.exp(s - m)
        denom = jnp.sum(p, axis=-1, keepdims=True) + jnp.exp(sink_l - m)
        ob = jnp.einsum('bkgqc,bckd->bqkgd', p / denom, vb.astype(jnp.float32))
        return ob.astype(h.dtype)

    outs = lax.map(block, jnp.arange(S // A_BLOCK))
    outs = jnp.moveaxis(outs, 0, 1).reshape(B, S, qw)
    return outs @ w_out


def conv_ffn(h, w_up, conv_w, conv_b, w_down):
    u = h @ w_up
    u = lax.conv_general_dilated(u, conv_w[:, None, :], window_strides=(1,),
                                 padding=((CONV_WIDTH // 2, CONV_WIDTH // 2),),
                                 dimension_numbers=('NWC', 'WIO', 'NWC'),
                                 feature_group_count=2 * D_FF) + conv_b
    g = u[..., :D_FF]
    val = u[..., D_FF:]
    return (jax.nn.silu(g) * val) @ w_down


def trunk(x, c, adaln_w, adaln_b, norm_g, mlstm_w_in, mlstm_b_gate, mlstm_head_g, mlstm_w_out,
          attn_w_in, attn_sink, attn_w_out, rel_bias, ffn_w_up, ffn_conv_w, ffn_conv_b, ffn_w_down,
          final_g):
    for l in range(DEPTH):
        mod = jax.nn.silu(c) @ adaln_w[l] + adaln_b[l]
        sh1, sc1, g1, sh2, sc2, g2 = [t[:, None, :] for t in jnp.split(mod, 6, axis=-1)]
        h = rmsnorm(x, norm_g[l, 0]) * (1 + sc1) + sh1
        if l % N_MIXERS == 0:
            j = l // N_MIXERS
            y = mlstm_mixer(h, mlstm_w_in[j], mlstm_b_gate[j], mlstm_head_g[j], mlstm_w_out[j])
        else:
            j = l // N_MIXERS
            y = swa_mixer(h, attn_w_in[j], attn_sink[j], rel_bias, attn_w_out[j])
        x = x + g1 * y
        h = rmsnorm(x, norm_g[l, 1]) * (1 + sc2) + sh2
        x = x + g2 * conv_ffn(h, ffn_w_up[l], ffn_conv_w[l], ffn_conv_b[l], ffn_w_down[l])
    return rmsnorm(x, final_g)


def setup_inputs(seed: int = 0) -> dict:
    key = jax.random.key(seed)
    ks = jax.random.split(key, 24)
    nrm = lambda k, shape, s: jax.random.normal(k, shape, jnp.float32) * s
    d = D_MODEL
    ig_bias = nrm(ks[6], (N_MLSTM_LAYERS, 2, 1, M_HEADS), 0.1)
    f_bias = 3.0 + 3.0 * jax.random.uniform(ks[7], (N_MLSTM_LAYERS, 2, 1, M_HEADS), jnp.float32)
    b_gate = jnp.concatenate([ig_bias, f_bias], axis=2).reshape(N_MLSTM_LAYERS, 4 * M_HEADS)
    return {
        'x_prompt': nrm(ks[0], (BATCH, SEQ, d), 1.0),
        'x_sample': nrm(ks[1], (DEC_BATCH, DEC_SEQ, d), 1.0),
        'c_prompt': nrm(ks[2], (BATCH, d), 1.0),
        'c_sample': nrm(ks[3], (DEC_BATCH, d), 1.0),
        'adaln_w': nrm(ks[4], (DEPTH, d, 6 * d), 0.5 * d ** -0.5),
        'adaln_b': nrm(ks[5], (DEPTH, 6 * d), 0.02),
        'norm_g': 1.0 + nrm(ks[8], (DEPTH, 2, d), 0.05),
        'mlstm_w_in': nrm(ks[9], (N_MLSTM_LAYERS, d, M_IN), d ** -0.5),
        'mlstm_b_gate': b_gate,
        'mlstm_head_g': 1.0 + nrm(ks[10], (N_MLSTM_LAYERS, M_HEADS, M_V_DIM), 0.05),
        'mlstm_w_out': nrm(ks[11], (N_MLSTM_LAYERS, M_HEADS * M_V_DIM, d), (M_HEADS * M_V_DIM) ** -0.5),
        'attn_w_in': nrm(ks[12], (N_ATTN_LAYERS, d, A_IN), d ** -0.5),
        'attn_sink': nrm(ks[13], (N_ATTN_LAYERS, A_HEADS), 0.5),
        'attn_w_out': nrm(ks[14], (N_ATTN_LAYERS, A_HEADS * A_HEAD_DIM, d), (A_HEADS * A_HEAD_DIM) ** -0.5),
        'rel_bias': nrm(ks[15], (NUM_BUCKETS, A_HEADS), 0.5),
        'ffn_w_up': nrm(ks[16], (DEPTH, d, 2 * D_FF), d ** -0.5),
        'ffn_conv_w': nrm(ks[17], (DEPTH, CONV_WIDTH, 2 * D_FF), CONV_WIDTH ** -0.5),
        'ffn_conv_b': nrm(ks[18], (DEPTH, 2 * D_FF), 0.02),
        'ffn_w_down': nrm(ks[19], (DEPTH, D_FF, d), D_FF ** -0.5),
        'final_g': 1.0 + nrm(ks[20], (d,), 0.05),
    }


def reference(x_prompt, x_sample, c_prompt, c_sample, adaln_w, adaln_b, norm_g, mlstm_w_in,
              mlstm_b_gate, mlstm_head_g, mlstm_w_out, attn_w_in, attn_sink, attn_w_out, rel_bias,
              ffn_w_up, ffn_conv_w, ffn_conv_b, ffn_w_down, final_g):
    y_prompt = trunk(x_prompt, c_prompt, adaln_w, adaln_b, norm_g, mlstm_w_in, mlstm_b_gate,
                     mlstm_head_g, mlstm_w_out, attn_w_in, attn_sink, attn_w_out, rel_bias,
                     ffn_w_up, ffn_conv_w, ffn_conv_b, ffn_w_down, final_g)
    y_sample = trunk(x_sample, c_sample, adaln_w, adaln_b, norm_g, mlstm_w_in, mlstm_b_gate,
                     mlstm_head_g, mlstm_w_out, attn_w_in, attn_sink, attn_w_out, rel_bias,
                     ffn_w_up, ffn_conv_w, ffn_conv_b, ffn_w_down, final_g)
    return (y_prompt, y_sample)
```

```python
import contextlib
import os
import numpy as np
import ml_dtypes
import concourse.bass as bass
import concourse.mybir as mybir
from concourse.bass_utils import run_bass_kernel_spmd

F32 = mybir.dt.float32
BF16 = mybir.dt.bfloat16
AF = mybir.ActivationFunctionType
ALU = mybir.AluOpType
AX = mybir.AxisListType

D = 1024
KD = 8
DFF = 2816
NFB = 22
MIN = 4112
EPS = 1e-6
NEG = -30000.0

COMPUTE = ("pe", "act", "dve", "pool")
ENGS = ("sp", "pe", "act", "dve", "pool")
NDMASEM = 24
NBG = 4


class Res:
    __slots__ = ("lw", "rd")

    def __init__(self):
        self.lw = None
        self.rd = []


class Op:
    __slots__ = ("eng", "fn", "deps", "sig", "val", "isdma", "sem", "ndma", "prev", "stage", "bg")


class Sched:
    def __init__(self, nc, stack):
        self.nc = nc
        self.ops = []
        self.stage = 0
        self.dma_rr = 0
        self.dma_last = [None] * NDMASEM
        self.dma_cnt = [0] * NDMASEM
        self.cnt = {e: 0 for e in COMPUTE}
        self.csem = {e: stack.enter_context(nc.semaphore("c_" + e)) for e in COMPUTE}
        self.dsem = [stack.enter_context(nc.semaphore("d_%d" % i)) for i in range(NDMASEM)]
        self.bsem = [stack.enter_context(nc.semaphore("b_%d" % i)) for i in range(NBG)]
        self.bcnt = [0] * NBG
        self.bgres = [Res() for _ in range(NBG)]
        self.waited = {e: {} for e in ENGS}
        self.ninst = {e: 0 for e in ENGS}

    def _deps(self, op, reads, writes):
        deps = []
        for r in reads:
            if r.lw is not None:
                deps.append((r.lw, 0))
        for w in writes:
            if w.lw is not None:
                deps.append((w.lw, 1))
            for q in w.rd:
                deps.append((q, 2))
        seen = set()
        for p, kind in deps:
            if p is op or id(p) in seen or (p.stage != self.stage and not p.bg):
                continue
            if not p.isdma and not op.isdma and p.eng == op.eng:
                if kind != 0 or p.eng == "pe":
                    continue
            seen.add(id(p))
            op.deps.append(p)
            p.sig = True
        for r in reads:
            r.rd.append(op)
        for w in writes:
            w.lw = op
            w.rd = []

    def _mk(self, eng, fn, isdma, ndma):
        o = Op()
        o.eng = eng; o.fn = fn; o.deps = []; o.sig = False; o.val = 0; o.isdma = isdma
        o.sem = None; o.ndma = ndma; o.prev = None; o.stage = self.stage; o.bg = False
        return o

    def op(self, eng, fn, reads=(), writes=()):
        o = self._mk(eng, fn, False, 0)
        self._deps(o, reads, writes)
        self.ops.append(o)
        return o

    def dma(self, eng, fn, reads=(), writes=(), n=1):
        o = self._mk(eng, fn, True, n)
        k = self.dma_rr
        self.dma_rr = (k + 1) % NDMASEM
        o.sem = k
        o.prev = self.dma_last[k]
        self.dma_cnt[k] += 16 * n
        o.val = self.dma_cnt[k]
        self.dma_last[k] = o
        o.sig = True
        self._deps(o, reads, writes)
        self.ops.append(o)
        return o

    def dma_bg(self, eng, fn, group, n=1, reads=()):
        o = self._mk(eng, fn, True, n)
        o.bg = True
        o.sem = group
        self.bcnt[group] += 16 * n
        o.val = self.bcnt[group]
        o.sig = True
        for r in reads:
            p = r.lw
            if p is not None and p.stage == self.stage and p not in o.deps:
                o.deps.append(p)
                p.sig = True
        self.bgres[group].lw = o
        self.ops.append(o)
        return o

    def flush(self):
        nc = self.nc
        last = {}
        for o in self.ops:
            if not o.isdma:
                last[o.eng] = o
        for o in last.values():
            o.sig = True
        for o in self.ops:
            if not o.isdma:
                if o.sig:
                    self.cnt[o.eng] += 1
                o.val = self.cnt[o.eng]
        per = {e: [o for o in self.ops if o.eng == e] for e in ENGS}
        csem, dsem = self.csem, self.dsem

        def run(ename, handle):
            waited = self.waited[ename]

            def wait(sem, key, val):
                if waited.get(key, 0) >= val:
                    return
                handle.wait_ge(sem, val)
                waited[key] = val

            for o in per[ename]:
                for p in o.deps:
                    if p.bg:
                        wait(self.bsem[p.sem], ("b", p.sem), self.bcnt[p.sem])
                    elif p.isdma:
                        wait(dsem[p.sem], ("d", p.sem), p.val)
                    else:
                        wait(csem[p.eng], ("c", p.eng), p.val)
                if o.bg:
                    o.fn(handle, self.bsem[o.sem])
                elif o.isdma:
                    if o.prev is not None:
                        wait(dsem[o.sem], ("d", o.sem), o.prev.val)
                    o.fn(handle, dsem[o.sem])
                else:
                    ins = o.fn(handle)
                    if o.sig:
                        ins.then_inc(csem[o.eng], 1)
                self.ninst[ename] += 1
            for k in range(NDMASEM):
                if self.dma_cnt[k]:
                    wait(dsem[k], ("d", k), self.dma_cnt[k])
            for e in COMPUTE:
                if self.cnt[e]:
                    wait(csem[e], ("c", e), self.cnt[e])

        with nc.Block() as block:
            block.sync(lambda h: run("sp", h))
            block.tensor(lambda h: run("pe", h))
            block.scalar(lambda h: run("act", h))
            block.vector(lambda h: run("dve", h))
            block.gpsimd(lambda h: run("pool", h))
        self.ops = []
        self.stage += 1


class Tl:
    def __init__(self, h, r=None):
        self.h = h
        self.r = r if r is not None else Res()

    def __getitem__(self, k):
        return self.h[k]


class Rot:
    def __init__(self, tiles):
        self.t = tiles
        self.i = 0

    def next(self):
        t = self.t[self.i % len(self.t)]
        self.i += 1
        return t


def _t5_bucket(rel):
    nb = 16
    max_exact = 8
    ret = np.where(rel > 0, nb, 0)
    n = np.abs(rel)
    nf = np.maximum(n, 1).astype(np.float32)
    large = max_exact + (np.log(nf / np.float32(max_exact)) / np.float32(np.log(128 / max_exact))
                         * np.float32(nb - max_exact)).astype(np.int32)
    large = np.minimum(large, nb - 1)
    return ret + np.where(n < max_exact, n, large)


def _consts():
    c = {}
    c["ident"] = np.eye(128, dtype=np.float32)
    s = np.arange(128)[:, None]
    t = np.arange(128)[None, :]
    c["tri"] = np.stack([(s <= t), (s >= t)]).astype(np.float32)
    c["antiI"] = np.ascontiguousarray(np.eye(128, dtype=np.float32)[::-1])
    oh = np.zeros((33, 512), np.float32)
    for rp in range(511):
        rel = rp - 255
        if abs(rel) <= 128:
            oh[int(_t5_bucket(np.array(rel))), rp] = 1.0
        else:
            oh[32, rp] = 1.0
    oh[32, 511] = 1.0
    c["oh"] = oh
    return c


def build(seqs, debug=False):
    NT = sum(seqs)
    starts = [sum(seqs[:i]) for i in range(len(seqs))]
    NS = len(seqs)
    nc = bass.Bass("TRN2", target_bir_lowering=False)
    dt_in = {}

    def din(name, shape):
        dt_in[name] = nc.dram_tensor(name, list(shape), F32, kind="ExternalInput")
        return dt_in[name]

    x_in = din("x", [NT, D])
    vecs_in = din("vecs", [512, 128])
    bc_in = din("bc", [128, 16 + 1024 + 16])
    adaln_w = din("adaln_w", [2, D, 6 * D])
    m_w_in = din("mlstm_w_in", [D, MIN])
    m_w_out = din("mlstm_w_out", [D, D])
    a_w_in = din("attn_w_in", [D, 1536])
    a_w_out = din("attn_w_out", [D, D])
    rel_bias = din("rel_bias", [32, 16])
    f_w_up = din("ffn_w_up", [2, D, 2 * DFF])
    f_w_dn = din("ffn_w_down", [2, DFF, D])
    c_ident = din("ident", [128, 128])
    c_tri = din("tri", [2, 128, 128])
    c_oh = din("oh", [33, 512])
    c_anti = din("antiI", [128, 128])
    y_out = nc.dram_tensor("y", [NT, D], F32, kind="ExternalOutput")
    skind = "ExternalOutput" if debug else "Internal"

    def dscr(name, shape, dt, kind="Internal"):
        return Tl(nc.dram_tensor(name, list(shape), dt, kind=kind))

    xa = dscr("xa", [KD, 128, NT], F32, skind)
    xb = dscr("xb", [KD, 128, NT], F32, skind)
    hfw = dscr("hfw", [NT, D], F32, skind)
    fd = dscr("fd", [16, 512], F32)
    wb_min = dscr("wb_min", [D, MIN], BF16)
    wb_mout = dscr("wb_mout", [D, D], BF16)
    wb_ain = dscr("wb_ain", [D, 1536], BF16)
    wb_aout = dscr("wb_aout", [D, D], BF16)
    wb_up = [dscr("wb_up%d" % l, [NFB, 128, KD, 256], BF16) for l in range(2)]
    wb_dn = [dscr("wb_dn%d" % l, [KD, 128, NFB, 128], BF16) for l in range(2)]

    top = contextlib.ExitStack()
    S = Sched(nc, top)

    uid = [0]

    def sb(st, name, shape, dt):
        uid[0] += 1
        return Tl(st.enter_context(nc.sbuf_tensor("%s_%d" % (name, uid[0]), list(shape), dt)))

    def dbg(name, ap, res, dt=F32):
        if not debug:
            return
        uid[0] += 1
        t = nc.dram_tensor("dbg_%s_%d" % (name, uid[0]), list(ap.shape), dt, kind="ExternalOutput")
        S.dma("sp", lambda e, s: e.dma_start(out=t.ap(), in_=ap).then_inc(s, 16), reads=[res])

    PB = [Tl(top.enter_context(nc.psum_tensor("pb%d" % i, [128, 512], F32))) for i in range(8)]

    def xT_dram(t, c0, c1):
        return t.h.ap()[:, :, c0:c1].rearrange("k p t -> p k t")

    identF = sb(top, "identF", [128, 128], F32)
    identB = sb(top, "identB", [128, 128], BF16)
    onesB = sb(top, "onesB", [128, 128], BF16)
    onesF = sb(top, "onesF", [128, 128], F32)
    triF = sb(top, "triF", [128, 2, 128], F32)
    vecsT = sb(top, "vecsT", [128, 512], F32)
    bcT = sb(top, "bcT", [128, 16 + 1024 + 16], F32)
    modT = sb(top, "modT", [128, 2, 6, KD, NS], F32)
    gsT = sb(top, "gsT", [128, 2, 2, KD, NS], F32)
    esink = sb(top, "esink", [128, 16], F32)
    zcol = sb(top, "zcol", [128, 1], F32)

    V_ADB, V_NG, V_FG, V_C, V_CW, V_CB = 0, 96, 128, 136, 152, 416

    def v_adb(l, j):
        return vecsT[:, V_ADB + (l * 6 + j) * 8: V_ADB + (l * 6 + j) * 8 + 8]

    def v_cw(l, t, f):
        c = V_CW + (l * 3 + t) * 44 + f
        return vecsT[:, c:c + 1]

    def v_cb(l, f):
        c = V_CB + l * 44 + f
        return vecsT[:, c:c + 1]

    def xpose_ops(st):
        xin_t = Rot([sb(st, "t_xin%d" % i, [128, D], F32) for i in range(3)])
        xo_t = Rot([sb(st, "t_xo%d" % i, [128, KD, 512], F32) for i in range(2)])
        bk = Rot([(PB[4], PB[5]), (PB[6], PB[7])])
        for blk in range(NT // 512):
            xo = xo_t.next()
            for cc in range(4):
                c = blk * 4 + cc
                xi = xin_t.next()
                S.dma("sp", lambda e, s, xi=xi, c=c: e.dma_start(out=xi[:], in_=x_in.ap()[c * 128:(c + 1) * 128, :]).then_inc(s, 16), writes=[xi.r])
                b0, b1 = bk.next()

                def tr(e, xi=xi, b0=b0, b1=b1):
                    ins = None
                    for k in range(KD):
                        ins = e.transpose((b0 if k < 4 else b1)[:, (k % 4) * 128:(k % 4) * 128 + 128], xi[:, k * 128:(k + 1) * 128], identF[:])
                    return ins
                S.op("pe", tr, reads=[xi.r, identF.r], writes=[b0.r, b1.r])
                S.op("act", lambda e, xo=xo, b0=b0, cc=cc: e.activation(xo[:, 0:4, cc * 128:(cc + 1) * 128], b0[:].rearrange("p (k t) -> p k t", k=4), AF.Copy), reads=[b0.r], writes=[xo.r])
                S.op("dve", lambda e, xo=xo, b1=b1, cc=cc: e.tensor_copy(xo[:, 4:8, cc * 128:(cc + 1) * 128], b1[:].rearrange("p (k t) -> p k t", k=4)), reads=[b1.r], writes=[xo.r])
            S.dma("sp", lambda e, s, xo=xo, blk=blk: e.dma_start(out=xT_dram(xa, blk * 512, (blk + 1) * 512), in_=xo[:]).then_inc(s, 16), reads=[xo.r], writes=[xa.r])

    castq = []

    def drip(k, res):
        for _ in range(k):
            if not castq:
                return
            grp, fn = castq.pop(0)
            S.dma_bg("pool", fn, grp, reads=[res])

    with contextlib.ExitStack() as st:
        S.dma("sp", lambda e, s: e.dma_start(out=identF[:], in_=c_ident.ap()).then_inc(s, 16), writes=[identF.r])
        S.dma("sp", lambda e, s: e.dma_start(out=triF[:], in_=c_tri.ap().rearrange("d s t -> s d t")).then_inc(s, 16),
              writes=[triF.r])
        S.dma("sp", lambda e, s: e.dma_start(out=bcT[:], in_=bc_in.ap()).then_inc(s, 16), writes=[bcT.r])
        S.op("act", lambda e: e.activation(identB[:], identF[:], AF.Copy), reads=[identF.r], writes=[identB.r])
        S.op("dve", lambda e: e.memset(onesB[:], 1.0), writes=[onesB.r])
        S.op("dve", lambda e: e.memset(onesF[:], 1.0), writes=[onesF.r])
        S.op("dve", lambda e: e.memset(zcol[:], 0.0), writes=[zcol.r])
        wres = Res()

        def cast_rows(dst, src, rows, nsplit):
            step = rows // nsplit
            for i in range(nsplit):
                S.dma("pool", lambda e, s, i=i: e.dma_start(out=dst.h.ap()[i * step:(i + 1) * step, :],
                                                            in_=src.ap()[i * step:(i + 1) * step, :]).then_inc(s, 16),
                      writes=[wres])


        vtmp = sb(st, "vtmp", [128, 4, 128], F32)
        S.dma("sp", lambda e, s: e.dma_start(out=vtmp[:], in_=vecs_in.ap().rearrange("(a r) c -> r a c", r=128)).then_inc(s, 16),
              writes=[vtmp.r])
        for a in range(4):
            S.op("pe", lambda e, a=a: e.transpose(PB[0][:, a * 128:(a + 1) * 128], vtmp[:, a, :], identF[:]),
                 reads=[vtmp.r, identF.r], writes=[PB[0].r])
        S.op("dve", lambda e: e.tensor_copy(vecsT[:], PB[0][:]), reads=[PB[0].r], writes=[vecsT.r])
        csT = sb(st, "csT", [128, KD, NS], F32)
        S.op("act", lambda e: e.activation(csT[:], vecsT[:, V_C:V_C + KD * NS].rearrange("p (k s) -> p k s", s=NS), AF.Silu),
             reads=[vecsT.r], writes=[csT.r])
        awt = Rot([sb(st, "awt%d" % i, [128, KD, 1024], F32) for i in range(2)])
        for l in range(2):
            for j in range(6):
                w = awt.next()
                S.dma("sp", lambda e, s, w=w, l=l, j=j: e.dma_start(
                    out=w[:], in_=adaln_w.ap()[l].rearrange("(k p) n -> p k n", p=128)[:, :, j * 1024:(j + 1) * 1024]
                ).then_inc(s, 16), writes=[w.r])
                pm = PB[1 + (l * 6 + j) % 2]

                def mm(e, w=w, pm=pm):
                    ins = None
                    for m in range(KD):
                        for k in range(KD):
                            ins = e.matmul(pm[:, m * NS:(m + 1) * NS], w[:, k, m * 128:(m + 1) * 128], csT[:, k, :],
                                           start=(k == 0), stop=(k == KD - 1))
                    return ins
                S.op("pe", mm, reads=[w.r, csT.r], writes=[pm.r])
                S.op("dve", lambda e, pm=pm, l=l, j=j: e.tensor_tensor(
                    modT[:, l, j, :, :], pm[:, 0:KD * NS].rearrange("p (m s) -> p m s", s=NS),
                    v_adb(l, j).unsqueeze(2).broadcast_to([128, KD, NS]), ALU.add),
                    reads=[pm.r, vecsT.r], writes=[modT.r])
        for l in range(2):
            for i in range(2):
                ng = vecsT[:, V_NG + (l * 2 + i) * 8: V_NG + (l * 2 + i) * 8 + 8]
                S.op("dve", lambda e, l=l, i=i, ng=ng: e.scalar_tensor_tensor(
                    gsT[:, l, i, :, :], modT[:, l, 1 + 3 * i, :, :], 1.0, ng.unsqueeze(2).broadcast_to([128, KD, NS]),
                    ALU.add, ALU.mult), reads=[modT.r, vecsT.r], writes=[gsT.r])
        S.op("act", lambda e: e.activation(esink[:], bcT[:, 1040:1056], AF.Exp), reads=[bcT.r], writes=[esink.r])
        rba = sb(st, "rba", [33, 16], F32)
        oht = sb(st, "oht", [33, 512], F32)
        fds = sb(st, "fds", [16, 512], F32)
        S.op("dve", lambda e: e.memset(rba[32:33, :], NEG), writes=[rba.r])
        S.dma("sp", lambda e, s: e.dma_start(out=rba[0:32, :], in_=rel_bias.ap()).then_inc(s, 16), reads=[rba.r], writes=[rba.r])
        S.dma("sp", lambda e, s: e.dma_start(out=oht[:], in_=c_oh.ap()).then_inc(s, 16), writes=[oht.r])
        S.op("pe", lambda e: e.matmul(PB[3][0:16, :], rba[:], oht[:], start=True, stop=True), reads=[rba.r, oht.r], writes=[PB[3].r])
        S.op("act", lambda e: e.activation(fds[:], PB[3][0:16, :], AF.Copy), reads=[PB[3].r], writes=[fds.r])
        S.dma("sp", lambda e, s: e.dma_start(out=fd.h.ap(), in_=fds[:]).then_inc(s, 16), reads=[fds.r], writes=[fd.r])
        def cast_rows_bg(dst, src, rows, nsplit, grp):
            step = rows // nsplit
            for i in range(nsplit):
                castq.append((grp, lambda e, s, i=i: e.dma_start(out=dst.h.ap()[i * step:(i + 1) * step, :],
                                                                 in_=src.ap()[i * step:(i + 1) * step, :]).then_inc(s, 16)))

        cast_rows_bg(wb_min, m_w_in, D, 8, 3)
        cast_rows_bg(wb_mout, m_w_out, D, 2, 3)
        for l in range(2):
            grp = 0 if l == 0 else 2
            for fb in range(NFB):
                for two in range(2):
                    castq.append((grp, lambda e, s, l=l, fb=fb, two=two: e.dma_start(
                        out=wb_up[l].h.ap()[fb][:, :, two * 128:(two + 1) * 128],
                        in_=f_w_up.ap()[l].rearrange("(k p) n -> p k n", p=128)[:, :, two * DFF + fb * 128:two * DFF + (fb + 1) * 128]
                    ).then_inc(s, 16)))
            for db in range(KD):
                castq.append((grp, lambda e, s, l=l, db=db: e.dma_start(
                    out=wb_dn[l].h.ap()[db],
                    in_=f_w_dn.ap()[l].rearrange("(f p) n -> p f n", p=128)[:, :, db * 128:(db + 1) * 128]
                ).then_inc(s, 16)))
            if l == 0:
                cast_rows_bg(wb_ain, a_w_in, D, 4, 1)
                cast_rows_bg(wb_aout, a_w_out, D, 2, 1)
        xpose_ops(st)
        if debug:
            dbg_mod = nc.dram_tensor("dbg_mod", [128, 2 * 6 * KD * NS], F32, kind="ExternalOutput")
            dbg_gs = nc.dram_tensor("dbg_gs", [128, 2 * 2 * KD * NS], F32, kind="ExternalOutput")
            dbg_vec = nc.dram_tensor("dbg_vec", [128, 512], F32, kind="ExternalOutput")
            S.dma("sp", lambda e, s: e.dma_start(out=dbg_mod.ap(), in_=modT[:].rearrange("p a b c d -> p (a b c d)")).then_inc(s, 16), reads=[modT.r])
            S.dma("sp", lambda e, s: e.dma_start(out=dbg_gs.ap(), in_=gsT[:].rearrange("p a b c d -> p (a b c d)")).then_inc(s, 16), reads=[gsT.r])
            S.dma("sp", lambda e, s: e.dma_start(out=dbg_vec.ap(), in_=vecsT[:]).then_inc(s, 16), reads=[vecsT.r])
        S.flush()

    def norm_block(st_tiles, xsrc, n, gs_ap, sh_ap, hout, hres, extra_reads, out_cols0=0):
        sq, lnv, tmpr, pss = st_tiles
        for k in range(KD):
            S.op("act", lambda e, k=k: e.activation(sq[:, k, 0:n], xsrc(k), AF.Square), reads=extra_reads, writes=[sq.r])

        def mm(e):
            ins = None
            for k in range(KD):
                ins = e.matmul(pss[:, 0:n], onesB[:], sq[:, k, 0:n], start=(k == 0), stop=(k == KD - 1))
            return ins
        S.op("pe", mm, reads=[sq.r, onesB.r], writes=[pss.r])
        S.op("act", lambda e: e.activation(lnv[:, 0:n], pss[:, 0:n], AF.Ln, bias=EPS, scale=1.0 / D), reads=[pss.r], writes=[lnv.r])
        S.op("act", lambda e: e.activation(lnv[:, 0:n], lnv[:, 0:n], AF.Exp, scale=-0.5), reads=[lnv.r], writes=[lnv.r])
        for k in range(KD):
            t = tmpr.next()
            S.op("dve", lambda e, k=k, t=t: e.tensor_tensor(t[:, 0:n], xsrc(k), lnv[:, 0:n], ALU.mult),
                 reads=list(extra_reads) + [lnv.r], writes=[t.r])
            if sh_ap is not None:
                S.op("act", lambda e, k=k, t=t: e.activation(hout[:, k, out_cols0:out_cols0 + n], t[:, 0:n], AF.Identity,
                                                              bias=sh_ap(k), scale=gs_ap(k)),
                     reads=[t.r, modT.r, gsT.r], writes=[hres])
            else:
                S.op("act", lambda e, k=k, t=t: e.activation(hout[:, k, out_cols0:out_cols0 + n], t[:, 0:n], AF.Copy,
                                                              scale=gs_ap(k)),
                     reads=[t.r, vecsT.r], writes=[hres])

    def ffn_stage(l, xin, xout, final):
        with contextlib.ExitStack() as st:
            xblk = Rot([sb(st, "f_x%d" % i, [128, KD, 513], F32) for i in range(2)])
            sq = sb(st, "f_sq", [128, KD, 512], BF16)
            lnv = sb(st, "f_ln", [128, 512], F32)
            tmpr = Rot([sb(st, "f_tmp%d" % i, [128, 512], F32) for i in range(2)])
            hTr = Rot([sb(st, "f_hT%d" % i, [128, KD, 512], BF16) for i in range(2)])
            wup = Rot([sb(st, "f_wu%d" % i, [128, KD, 256], BF16) for i in range(5)])
            ug = Rot([sb(st, "f_ug%d" % i, [128, 514], F32) for i in range(2)])
            uv = Rot([sb(st, "f_uv%d" % i, [128, 514], F32) for i in range(2)])
            tg = Rot([sb(st, "f_tg%d" % i, [128, 512], F32) for i in range(3)])
            tv = Rot([sb(st, "f_tv%d" % i, [128, 512], F32) for i in range(3)])
            sg = Rot([sb(st, "f_sg%d" % i, [128, 512], F32) for i in range(2)])
            actr = Rot([sb(st, "f_act%d" % i, [128, NFB, 512], BF16) for i in range(2)])
            wdn = Rot([sb(st, "f_wd%d" % i, [128, NFB, 128], BF16) for i in range(4)])
            carry = sb(st, "f_carry", [128, 2 * NFB, 2], F32)
            Er = Rot([sb(st, "f_E%d" % i, [128, 2 * NFB, 4], F32) for i in range(2)])
            Ez = sb(st, "f_Ez", [128, 2 * NFB, 4], F32)
            S.op("pool", lambda e: e.memset(Ez[:], 0.0), writes=[Ez.r])
            epr = Rot([sb(st, "f_ep%d" % i, [128, 2 * NFB, 2], F32) for i in range(2)])
            etm = Rot([sb(st, "f_et%d" % i, [128, 2 * NFB], F32) for i in range(2)])
            esr = Rot([sb(st, "f_es%d" % i, [128, NFB, 2], F32) for i in range(2)])
            lastE = [None]
            if final:
                yn = sb(st, "f_yn", [128, KD, 512], F32)
                ytok = Rot([sb(st, "f_yt%d" % i, [128, D], F32) for i in range(2)])
            pg = Rot([PB[0], PB[1]])
            pv = Rot([PB[2], PB[3]])
            py = Rot([PB[4], PB[5]])
            pss = PB[6]
            tailq = []
            storeq = []
            finq = []

            def flush_tail():
                while tailq:
                    tailq.pop(0)()

            def flush_stores():
                while storeq:
                    storeq.pop(0)()

            class Blk:
                pass

            def load_norm(si, s0, nb, i):
                b = Blk()
                b.si, b.i = si, i
                b.tail = (i == nb)
                b.n = 1 if b.tail else 512
                b.p = s0 + 512 * i
                b.c0 = 1 if i == 0 else 0
                b.xt = xblk.next()
                b.hT = None
                b.act = actr.next()
                b.Eprev = lastE[0] if i > 0 else Ez
                b.E = None if b.tail else Er.next()
                lastE[0] = b.E
                xt, c0, p, n = b.xt, b.c0, b.p, b.n
                hi = 1 if b.tail else n + 1
                S.dma("sp", lambda e, s: e.dma_start(out=xt[:, :, c0:hi], in_=xT_dram(xin, p - 1 + c0, p - 1 + hi), allow_slow_non_contiguous=True).then_inc(s, 16),
                      reads=[xin.r], writes=[xt.r])
                if not b.tail:
                    b.hT = hTr.next()
                    norm_block((sq, lnv, tmpr, pss), lambda k: xt[:, k, 1:1 + n], n, lambda k: gsT[:, l, 1, k, si:si + 1],
                               lambda k: modT[:, l, 3, k, si:si + 1], b.hT, b.hT.r, [xt.r])
                return b

            def up(b, fbs, hook=None):
                n, hT, act = b.n, b.hT, b.act
                if b.tail:
                    flush_tail()
                    return
                E = b.E
                LO = 2
                for fb in fbs:
                    g_t, v_t = ug.next(), uv.next()
                    a_t, b_t = tg.next(), tv.next()
                    chains = ((g_t, a_t, fb), (v_t, b_t, NFB + fb))
                    wt = wup.next()
                    S.dma("sp", lambda e, s, wt=wt, fb=fb: e.dma_start(out=wt[:], in_=wb_up[l].h.ap()[fb]).then_inc(s, 16), reads=[S.bgres[0 if l == 0 else 2]], writes=[wt.r])
                    for (pt, ut, ot, off, f) in ((pg.next(), g_t, a_t, 0, fb), (pv.next(), v_t, b_t, 128, NFB + fb)):
                        def mm(e, wt=wt, pt=pt, off=off):
                            ins = None
                            for k in range(KD):
                                ins = e.matmul(pt[:, 0:n], wt[:, k, off:off + 128], hT[:, k, 0:n], start=(k == 0), stop=(k == KD - 1))
                            return ins
                        S.op("pe", mm, reads=[wt.r, hT.r], writes=[pt.r])
                        S.op("act", lambda e, pt=pt, ut=ut: e.activation(ut[:, 2:2 + n], pt[:, 0:n], AF.Copy), reads=[pt.r], writes=[ut.r])
                        L1 = max(LO, 1)
                        S.op("act", lambda e, pt=pt, ot=ot, f=f, L1=L1: e.activation(ot[:, L1:n], pt[:, L1 - 1:n - 1], AF.Identity, bias=v_cb(l, f), scale=v_cw(l, 1, f)),
                             reads=[pt.r, vecsT.r], writes=[ot.r])
                    for (ut, ot, f) in chains:
                        S.op("pool", lambda e, ut=ut, f=f: e.tensor_copy(E[:, f, 0:2], ut[:, 2:4]), reads=[ut.r], writes=[E.r])
                        S.op("pool", lambda e, ut=ut, f=f: e.tensor_copy(E[:, f, 2:4], ut[:, n:n + 2]), reads=[ut.r], writes=[E.r])
                    flush_tail()
                    for (ut, ot, f) in chains:
                        S.op("dve", lambda e, ut=ut, ot=ot, f=f: e.scalar_tensor_tensor(ot[:, LO:n], ut[:, LO:n], v_cw(l, 0, f), ot[:, LO:n], ALU.mult, ALU.add),
                             reads=[ut.r, ot.r, vecsT.r], writes=[ot.r])
                    for (ut, ot, f) in chains:
                        S.op("dve", lambda e, ut=ut, ot=ot, f=f: e.scalar_tensor_tensor(ot[:, LO:n], ut[:, LO + 2:2 + n], v_cw(l, 2, f), ot[:, LO:n], ALU.mult, ALU.add),
                             reads=[ut.r, ot.r, vecsT.r], writes=[ot.r])

                    def tail_ops(a_t=a_t, b_t=b_t, fb=fb):
                        s_t = sg.next()
                        S.op("act", lambda e: e.activation(s_t[:, LO:n], a_t[:, LO:n], AF.Silu), reads=[a_t.r], writes=[s_t.r])
                        S.op("pool", lambda e: e.tensor_tensor(act[:, fb, LO:n], s_t[:, LO:n], b_t[:, LO:n], ALU.mult), reads=[s_t.r, b_t.r], writes=[act.r])
                    tailq.append(tail_ops)
                    if fb == 12:
                        flush_stores()
                    if fb == 14 and hook is not None:
                        hook()
                    if fb == 8:
                        while finq:
                            finq.pop(0)()

            def edge(b):
                act = b.act
                C = b.Eprev
                Nn = b.E if not b.tail else Ez
                ep, tm, se = epr.next(), etm.next(), esr.next()
                W = [vecsT[:, V_CW + (l * 3 + t) * 44: V_CW + (l * 3 + t) * 44 + 44] for t in range(3)]
                Bv = vecsT[:, V_CB + l * 44: V_CB + l * 44 + 44]
                for col, (x0, x1, x2) in enumerate(((C[:, :, 2], C[:, :, 3], Nn[:, :, 0]), (C[:, :, 3], Nn[:, :, 0], Nn[:, :, 1]))):
                    o = ep[:, :, col]
                    rd = [C.r, Nn.r, vecsT.r, ep.r, tm.r]
                    S.op("pool", lambda e, o=o, x0=x0: e.tensor_tensor(o, W[0], x0, ALU.mult), reads=rd, writes=[ep.r])
                    S.op("pool", lambda e, x1=x1: e.tensor_tensor(tm[:], W[1], x1, ALU.mult), reads=rd, writes=[tm.r])
                    S.op("pool", lambda e, o=o: e.tensor_tensor(o, o, tm[:], ALU.add), reads=rd, writes=[ep.r])
                    S.op("pool", lambda e, x2=x2: e.tensor_tensor(tm[:], W[2], x2, ALU.mult), reads=rd, writes=[tm.r])
                    S.op("pool", lambda e, o=o: e.tensor_tensor(o, o, tm[:], ALU.add), reads=rd, writes=[ep.r])
                    S.op("pool", lambda e, o=o: e.tensor_tensor(o, o, Bv, ALU.add), reads=rd, writes=[ep.r])
                S.op("act", lambda e: e.activation(se[:], ep[:, 0:NFB, :], AF.Silu), reads=[ep.r], writes=[se.r])
                S.op("pool", lambda e: e.tensor_tensor(act[:, :, 0:2], se[:], ep[:, NFB:2 * NFB, :], ALU.mult), reads=[se.r, ep.r], writes=[act.r])

            def down(b):
                n, xt, act, si, c0, p = b.n, b.xt, b.act, b.si, b.c0, b.p
                if debug and l == 0 and b.si == 1 and b.i == 0:
                    dbg("act", act[:], act.r, BF16)
                for db in range(KD):
                    wd = wdn.next()
                    S.dma("sp", lambda e, s, wd=wd, db=db: e.dma_start(out=wd[:], in_=wb_dn[l].h.ap()[db]).then_inc(s, 16), reads=[S.bgres[0 if l == 0 else 2]], writes=[wd.r])
                    pt = py.next()

                    def mm(e, wd=wd, pt=pt):
                        ins = None
                        for f in range(NFB):
                            ins = e.matmul(pt[:, 0:n], wd[:, f, :], act[:, f, 0:n], start=(f == 0), stop=(f == NFB - 1))
                        return ins
                    S.op("pe", mm, reads=[wd.r, act.r], writes=[pt.r])
                    S.op("dve", lambda e, pt=pt, db=db: e.scalar_tensor_tensor(
                        xt[:, db, 0:n], pt[:, 0:n], modT[:, l, 5, db, si:si + 1], xt[:, db, 0:n], ALU.mult, ALU.add),
                        reads=[pt.r, xt.r, modT.r], writes=[xt.r])
                tok0 = p - 1 + c0
                if not final:
                    storeq.append(lambda: S.dma("sp", lambda e, s: e.dma_start(out=xT_dram(xout, tok0, tok0 + n - c0), in_=xt[:, :, c0:n],
                                                                               allow_slow_non_contiguous=True).then_inc(s, 16),
                                                reads=[xt.r], writes=[xout.r]))
                    return
                finq.append(lambda: final_part(n, xt, c0, p))

            def final_part(n, xt, c0, p):
                norm_block((sq, lnv, tmpr, pss), lambda k: xt[:, k, 0:n], n, lambda k: vecsT[:, V_FG + k:V_FG + k + 1], None, yn, yn.r, [xt.r])
                for g0 in range(0, n, 128):
                    wdt = min(128, n - g0)
                    banks = [py.next(), py.next()]

                    def tr(e, g0=g0, wdt=wdt, banks=banks):
                        ins = None
                        for k in range(KD):
                            ins = e.transpose(banks[k // 4][0:wdt, (k % 4) * 128:(k % 4) * 128 + 128], yn[:, k, g0:g0 + wdt], identF[:])
                        return ins
                    S.op("pe", tr, reads=[yn.r, identF.r], writes=[banks[0].r, banks[1].r])
                    yt = ytok.next()
                    S.op("act", lambda e, yt=yt, banks=banks, wdt=wdt: e.activation(yt[0:wdt, 0:512], banks[0][0:wdt, :], AF.Copy),
                         reads=[banks[0].r], writes=[yt.r])
                    S.op("dve", lambda e, yt=yt, banks=banks, wdt=wdt: e.tensor_copy(yt[0:wdt, 512:1024], banks[1][0:wdt, :]),
                         reads=[banks[1].r], writes=[yt.r])
                    r0 = c0 if g0 == 0 else 0
                    t0 = p - 1 + g0 + r0
                    if wdt - r0 > 0:
                        S.dma("sp", lambda e, s, yt=yt, r0=r0, wdt=wdt, t0=t0: e.dma_start(
                            out=y_out.ap()[t0:t0 + wdt - r0, :], in_=yt[r0:wdt, :]).then_inc(s, 16), reads=[yt.r])

            for si, (s0, SL) in enumerate(zip(starts, seqs)):
                nb = SL // 512
                flush_tail()
                S.op("pool", lambda e: e.memset(carry[:], 0.0), writes=[carry.r])
                NPRE = 6
                cur = load_norm(si, s0, nb, 0)
                for i in range(nb + 1):
                    holder = []
                    hook = (lambda i=i, holder=holder: holder.append(load_norm(si, s0, nb, i + 1))) if i < nb else None
                    up(cur, range(NFB) if i == 0 else range(NPRE, NFB), hook)
                    edge(cur)
                    if i < nb:
                        nxt = holder[0]
                        up(nxt, range(0, NPRE))
                        down(cur)
                        cur = nxt
                flush_tail()
                down(cur)
                flush_stores()
                while finq:
                    finq.pop(0)()
            S.flush()

    def final_stage(xin):
        with contextlib.ExitStack() as st:
            xr = Rot([sb(st, "o_x%d" % i, [128, KD, 512], F32) for i in range(3)])
            sq = sb(st, "o_sq", [128, KD, 512], BF16)
            lnv = sb(st, "o_ln", [128, 512], F32)
            tmpr = Rot([sb(st, "o_tmp%d" % i, [128, 512], F32) for i in range(2)])
            ynr = Rot([sb(st, "o_yn%d" % i, [128, KD, 512], F32) for i in range(2)])
            ytok = Rot([sb(st, "o_yt%d" % i, [128, D], F32) for i in range(4)])
            bankr = Rot([(PB[0], PB[1]), (PB[2], PB[3]), (PB[4], PB[5])])
            pss = PB[6]

            def load_norm(blk):
                xt, yn = xr.next(), ynr.next()
                S.dma("sp", lambda e, s: e.dma_start(out=xt[:], in_=xT_dram(xin, blk * 512, (blk + 1) * 512)).then_inc(s, 16),
                      reads=[xin.r], writes=[xt.r])
                norm_block((sq, lnv, tmpr, pss), lambda k: xt[:, k, :], 512, lambda k: vecsT[:, V_FG + k:V_FG + k + 1], None, yn, yn.r, [xt.r])
                return yn

            def emit_out(blk, yn):
                for g in range(4):
                    banks = bankr.next()

                    def tr(e, g=g, banks=banks):
                        ins = None
                        for k in range(KD):
                            ins = e.transpose(banks[k // 4][:, (k % 4) * 128:(k % 4) * 128 + 128], yn[:, k, g * 128:(g + 1) * 128], identF[:])
                        return ins
                    S.op("pe", tr, reads=[yn.r, identF.r], writes=[banks[0].r, banks[1].r])
                    yt = ytok.next()
                    S.op("act", lambda e, yt=yt, banks=banks: e.activation(yt[:, 0:512], banks[0][:, :], AF.Copy), reads=[banks[0].r], writes=[yt.r])
                    S.op("dve", lambda e, yt=yt, banks=banks: e.tensor_copy(yt[:, 512:1024], banks[1][:, :]), reads=[banks[1].r], writes=[yt.r])
                    t0 = blk * 512 + g * 128
                    S.dma("sp", lambda e, s, yt=yt, t0=t0: e.dma_start(out=y_out.ap()[t0:t0 + 128, :], in_=yt[:]).then_inc(s, 16), reads=[yt.r])

            nblk = NT // 512
            prev = None
            for blk in range(nblk + 1):
                cur = (blk, load_norm(blk)) if blk < nblk else None
                if prev is not None:
                    emit_out(*prev)
                prev = cur
            S.flush()

    def attn_stage(xin, xout):
        l = 1
        with contextlib.ExitStack() as st:
            wq = sb(st, "a_wq", [128, KD, 1024], BF16)
            wkd = sb(st, "a_wkd", [128, KD, 4, 2, 128], BF16)
            wv = sb(st, "a_wv", [128, KD, 256], BF16)
            wo = sb(st, "a_wo", [128, KD, 1024], BF16)
            biasT = sb(st, "a_bias", [128, 3, 16, 128], F32)
            xkv = Rot([sb(st, "a_x%d" % i, [128, KD, 768], F32) for i in range(1)])
            sq = sb(st, "a_sq", [128, KD, 512], BF16)
            lnv = sb(st, "a_ln", [128, 512], F32)
            tmpr = Rot([sb(st, "a_tmp%d" % i, [128, 512], F32) for i in range(2)])
            hT = sb(st, "a_hT", [128, KD, 768], BF16)
            QT = sb(st, "a_QT", [128, KD, 512], BF16)
            KT2 = sb(st, "a_KT", [128, 4, 2, 768], BF16)
            Vaug = sb(st, "a_V", [128, 6, 4, 66], BF16)
            tS = Rot([sb(st, "a_ts%d" % i, [128, 512], F32) for i in range(6)])
            pT = Rot([sb(st, "a_pT%d" % i, [128, 4, 128], BF16) for i in range(7)])
            den4 = Rot([sb(st, "a_den%d" % i, [128, 4], F32) for i in range(3)])
            r4 = Rot([sb(st, "a_r%d" % i, [128, 4], F32) for i in range(3)])
            AO = Rot([sb(st, "a_AO%d" % i, [128, D], BF16) for i in range(2)])
            AOT = sb(st, "a_AOT", [128, KD, 512], BF16)
            pproj = Rot([PB[0], PB[1]])
            pscore = Rot([PB[2], PB[3], PB[0], PB[1]])
            pacc = Rot([PB[4], PB[5]])
            pss = PB[6]
            ptr = PB[7]
            wsrc = wb_ain.h.ap().rearrange("(k p) n -> p k n", p=128)
            S.dma("sp", lambda e, s: e.dma_start(out=wq[:], in_=wsrc[:, :, 0:1024]).then_inc(s, 16), reads=[S.bgres[1]], writes=[wq.r])
            S.op("pool", lambda e: e.memset(wkd[:], 0.0), writes=[wkd.r])
            for g in range(4):
                for dup in range(2):
                    S.dma("sp", lambda e, s, g=g, dup=dup: e.dma_start(out=wkd[:, :, g, dup, dup * 64:(dup + 1) * 64],
                                                                       in_=wsrc[:, :, 1024 + g * 64:1024 + (g + 1) * 64]).then_inc(s, 16),
                          reads=[wkd.r, S.bgres[1]], writes=[wkd.r])
            S.dma("sp", lambda e, s: e.dma_start(out=wv[:], in_=wsrc[:, :, 1280:1536]).then_inc(s, 16), reads=[S.bgres[1]], writes=[wv.r])
            S.dma("sp", lambda e, s: e.dma_start(out=wo[:], in_=wb_aout.h.ap().rearrange("(k p) n -> p k n", p=128)).then_inc(s, 16), reads=[S.bgres[1]], writes=[wo.r])
            antiI = sb(st, "a_J", [128, 128], F32)
            hank = Rot([sb(st, "a_hk%d" % i, [128, 16, 128], F32) for i in range(1)])
            S.dma("sp", lambda e, s: e.dma_start(out=antiI[:], in_=c_anti.ap()).then_inc(s, 16), writes=[antiI.r])
            for rb in range(3):
                hk = hank.next()
                S.dma("sp", lambda e, s, rb=rb, hk=hk: e.dma_start(out=hk[:], in_=bass.AP(fd.h, rb * 128, [[1, 128], [512, 16], [1, 128]])).then_inc(s, 16),
                      writes=[hk.r])
                for h4 in range(4):
                    pt = pproj.next()

                    def mmb(e, hk=hk, h4=h4, pt=pt):
                        ins = None
                        for j in range(4):
                            ins = e.matmul(pt[:, j * 128:(j + 1) * 128], hk[:, h4 * 4 + j, :], antiI[:], start=True, stop=True)
                        return ins
                    S.op("pe", mmb, reads=[hk.r, antiI.r], writes=[pt.r])
                    S.op("act", lambda e, rb=rb, h4=h4, pt=pt: e.activation(biasT[:, rb, h4 * 4:h4 * 4 + 4, :], pt[:].rearrange("p (h q) -> p h q", h=4), AF.Copy),
                         reads=[pt.r], writes=[biasT.r])
            S.op("pool", lambda e: e.memset(Vaug[:], 1.0), writes=[Vaug.r])
            CUT = 9.0
            for si, (s0, SL) in enumerate(zip(starts, seqs)):
                if CUT <= 1:
                    break
                gs_ap = lambda k, si=si: gsT[:, l, 0, k, si:si + 1]
                sh_ap = lambda k, si=si: modT[:, l, 0, k, si:si + 1]

                def blk(i, si=si, s0=s0, SL=SL, gs_ap=gs_ap, sh_ap=sh_ap):
                    p = s0 + 512 * i
                    kc0 = max(p - 128, s0)
                    kc1 = min(p + 640, s0 + SL)
                    nk = kc1 - kc0
                    nkb = nk // 128
                    qoff = p - kc0
                    xt = xkv.next()
                    S.dma("sp", lambda e, s: e.dma_start(out=xt[:, :, 0:nk], in_=xT_dram(xin, kc0, kc1)).then_inc(s, 16),
                          reads=[xin.r], writes=[xt.r])
                    for c in range(0, nk, 512):
                        n_ = min(512, nk - c)
                        norm_block((sq, lnv, tmpr, pss), lambda k, c=c, n_=n_: xt[:, k, c:c + n_], n_, gs_ap, sh_ap, hT, hT.r, [xt.r], out_cols0=c)
                    for ob in range(KD):
                        pt = pproj.next()

                        def mmq(e, ob=ob, pt=pt):
                            ins = None
                            for k in range(KD):
                                ins = e.matmul(pt[:, 0:512], wq[:, k, ob * 128:(ob + 1) * 128], hT[:, k, qoff:qoff + 512], start=(k == 0), stop=(k == KD - 1))
                            return ins
                        S.op("pe", mmq, reads=[wq.r, hT.r], writes=[pt.r])
                        S.op("act", lambda e, ob=ob, pt=pt: e.activation(QT[:, ob, :], pt[:, 0:512], AF.Copy, scale=0.125), reads=[pt.r], writes=[QT.r])
                    for g in range(4):
                      for hp in range(2):
                        for c in range(0, nk, 512):
                            n_ = min(512, nk - c)
                            pt = pproj.next()

                            def mmk(e, g=g, hp=hp, c=c, n_=n_, pt=pt):
                                ins = None
                                for k in range(KD):
                                    ins = e.matmul(pt[:, 0:n_], wkd[:, k, g, hp, :], hT[:, k, c:c + n_], start=(k == 0), stop=(k == KD - 1))
                                return ins
                            S.op("pe", mmk, reads=[wkd.r, hT.r], writes=[pt.r])
                            S.op("dve", lambda e, g=g, hp=hp, c=c, n_=n_, pt=pt: e.tensor_copy(KT2[:, g, hp, c:c + n_], pt[:, 0:n_]), reads=[pt.r], writes=[KT2.r])
                    for kb in range(nkb):
                        pt = pproj.next()

                        def mmv(e, kb=kb, pt=pt):
                            ins = None
                            for k in range(KD):
                                ins = e.matmul(pt[:, 0:256], hT[:, k, kb * 128:(kb + 1) * 128], wv[:, k, :], start=(k == 0), stop=(k == KD - 1))
                            return ins
                        S.op("pe", mmv, reads=[wv.r, hT.r], writes=[pt.r])
                        S.op("act" if kb % 2 == 0 else "dve",
                             (lambda e, kb=kb, pt=pt: e.activation(Vaug[:, kb, :, 0:64], pt[:, 0:256].rearrange("p (g d) -> p g d", g=4), AF.Copy)) if kb % 2 == 0 else
                             (lambda e, kb=kb, pt=pt: e.tensor_copy(Vaug[:, kb, :, 0:64], pt[:, 0:256].rearrange("p (g d) -> p g d", g=4))),
                             reads=[pt.r], writes=[Vaug.r])
                    if CUT <= 2:
                        return
                    LAG = 4
                    pend = []

                    class _Q:
                        def append_m(self, f):
                            pend.append(("m", f))

                        def append_f(self, f):
                            pend.append(("f", f))
                    pend_mmo = pend_fin = None

                    def run_pending(keep=0):
                        while pend and (sum(1 for k_, _ in pend if k_ == "m") > keep or pend[0][0] == "f"):
                            pend.pop(0)[1]()

                    for qb in range(4):
                        kb_lo = qoff // 128 + qb - 1
                        valid = [rb for rb in range(3) if 0 <= kb_lo + rb < nkb]
                        ao = AO.next()
                        for g in range(4):
                            po = pacc.next()
                            for idx, rb in enumerate(valid):
                                kbl = kb_lo + rb
                                ps = pscore.next()

                                def mms(e, g=g, kbl=kbl, ps=ps, qb=qb):
                                    ins = None
                                    for i4 in range(4):
                                        h = 4 * g + i4
                                        ob, hp = h // 2, h % 2
                                        ins = e.matmul(ps[:, i4 * 128:(i4 + 1) * 128], KT2[:, g, hp, kbl * 128:(kbl + 1) * 128],
                                                       QT[:, ob, qb * 128:(qb + 1) * 128], start=True, stop=True)
                                    return ins
                                S.op("pe", mms, reads=[KT2.r, QT.r], writes=[ps.r])
                                ts = tS.next()
                                S.op("dve", lambda e, g=g, rb=rb, ps=ps, ts=ts: e.tensor_tensor(
                                    ts[:], ps[:], biasT[:, rb, 4 * g:4 * g + 4, :].rearrange("p h q -> p (h q)"), ALU.add),
                                    reads=[ps.r, biasT.r], writes=[ts.r])
                                pt_ = pT.next()
                                S.op("act", lambda e, ts=ts, pt_=pt_: e.activation(pt_[:].rearrange("p h q -> p (h q)"), ts[:], AF.Exp), reads=[ts.r], writes=[pt_.r])
                                run_pending(LAG - 1)

                                def mmo(e, g=g, kbl=kbl, po=po, pt_=pt_, idx=idx, last=(idx == len(valid) - 1)):
                                    ins = None
                                    for i4 in range(4):
                                        ins = e.matmul(po[:, i4 * 65:i4 * 65 + 65], pt_[:, i4, :], Vaug[:, kbl, g, 0:65],
                                                       start=(idx == 0 and i4 == 0), stop=last, skip_group_check=True)
                                    return ins
                                pend.append(("m", lambda mmo=mmo, pt_=pt_, po=po: S.op("pe", mmo, reads=[pt_.r, Vaug.r], writes=[po.r])))

                            def fin(g=g, po=po, ao=ao):
                                dn, rr = den4.next(), r4.next()
                                pov = po[:, 0:260].rearrange("p (h d) -> p h d", h=4)
                                S.op("dve", lambda e: e.tensor_tensor(dn[:], pov[:, :, 64], esink[:, 4 * g:4 * g + 4], ALU.add),
                                     reads=[po.r, esink.r], writes=[dn.r])
                                S.op("dve", lambda e: e.reciprocal(rr[:], dn[:]), reads=[dn.r], writes=[rr.r])
                                S.op("dve", lambda e: e.tensor_tensor(
                                    ao[:, g * 256:(g + 1) * 256].rearrange("p (h d) -> p h d", h=4), pov[:, :, 0:64],
                                    rr[:].unsqueeze(2).broadcast_to([128, 4, 64]), ALU.mult), reads=[po.r, rr.r], writes=[ao.r])
                            pend.append(("f", fin))

                        def trq(qb=qb, ao=ao):
                            ptb = ptr[:].bitcast(BF16)

                            def trp(e):
                                ins = None
                                for k in range(KD):
                                    ins = e.transpose(ptb[:, k * 128:(k + 1) * 128], ao[:, k * 128:(k + 1) * 128], identB[:])
                                return ins
                            S.op("pe", trp, reads=[ao.r, identB.r], writes=[ptr.r])
                            S.op("act", lambda e: e.activation(AOT[:, :, qb * 128:(qb + 1) * 128], ptb[:, 0:1024].rearrange("p (k t) -> p k t", k=KD), AF.Copy),
                                 reads=[ptr.r], writes=[AOT.r])
                        pend.append(("f", trq))
                    run_pending(0)
                    if CUT <= 3:
                        return
                    for db in range(KD):
                        pt = pproj.next()

                        def mmp(e, db=db, pt=pt):
                            ins = None
                            for k in range(KD):
                                ins = e.matmul(pt[:, 0:512], wo[:, k, db * 128:(db + 1) * 128], AOT[:, k, :], start=(k == 0), stop=(k == KD - 1))
                            return ins
                        S.op("pe", mmp, reads=[wo.r, AOT.r], writes=[pt.r])
                        S.op("dve", lambda e, db=db, pt=pt: e.scalar_tensor_tensor(
                            xt[:, db, qoff:qoff + 512], pt[:, 0:512], modT[:, l, 2, db, si:si + 1], xt[:, db, qoff:qoff + 512], ALU.mult, ALU.add),
                            reads=[pt.r, xt.r, modT.r], writes=[xt.r])
                    S.dma("sp", lambda e, s: e.dma_start(out=xT_dram(xout, p, p + 512), in_=xt[:, :, qoff:qoff + 512]).then_inc(s, 16),
                          reads=[xt.r], writes=[xout.r])
                for i in range(SL // 512):
                    blk(i)
            S.flush()

    def mlstm_pass(d):
        l = 0
        TB = 256
        with contextlib.ExitStack() as st:
            w = sb(st, "m_w", [128, KD, 3088], BF16)
            wsrc = wb_min.h.ap().rearrange("(k p) n -> p k n", p=128)
            if d == 1:
                S.dma("sp", lambda e, s: e.dma_start(out=w[:, :, 0:3072], in_=wsrc[:, :, 0:3072]).then_inc(s, 16), reads=[S.bgres[3]], writes=[w.r])
                S.dma("sp", lambda e, s: e.dma_start(out=w[:, :, 3072:3088], in_=wsrc[:, :, 4096:4112]).then_inc(s, 16), reads=[w.r, S.bgres[3]], writes=[w.r])
            else:
                stg = Rot([sb(st, "m_stg%d" % i, [128, KD, 512], F32) for i in range(2)])
                fsrc = m_w_in.ap().rearrange("(k p) n -> p k n", p=128)
                wparts = [Res() for _ in range(7)]
                for ci in range(7):
                    sg_ = stg.next()
                    c0, c1, o0 = (ci * 512, ci * 512 + 512, ci * 512) if ci < 6 else (4096, 4112, 3072)
                    S.dma("sp", lambda e, s, sg_=sg_, c0=c0, c1=c1: e.dma_start(out=sg_[:, :, 0:c1 - c0], in_=fsrc[:, :, c0:c1]).then_inc(s, 16), writes=[sg_.r])
                    if ci % 2 == 0:
                        S.op("act", lambda e, sg_=sg_, c0=c0, c1=c1, o0=o0: e.activation(w[:, :, o0:o0 + c1 - c0], sg_[:, :, 0:c1 - c0], AF.Copy), reads=[sg_.r], writes=[w.r])
                    else:
                        S.op("dve", lambda e, sg_=sg_, c0=c0, c1=c1, o0=o0: e.tensor_copy(w[:, :, o0:o0 + c1 - c0], sg_[:, :, 0:c1 - c0]), reads=[sg_.r], writes=[w.r])
            if d == 1:
                wo_in = sb(st, "m_woi", [128, KD, 1024], BF16)
                wout = sb(st, "m_wout", [128, KD, 1024], BF16)
                S.dma("sp", lambda e, s: e.dma_start(out=wo_in[:], in_=wsrc[:, :, 3072:4096]).then_inc(s, 16), reads=[S.bgres[3]], writes=[wo_in.r])
                S.dma("sp", lambda e, s: e.dma_start(out=wout[:], in_=wb_mout.h.ap().rearrange("(k p) n -> p k n", p=128)).then_inc(s, 16), reads=[S.bgres[3]], writes=[wout.r])
            xTr = Rot([sb(st, "m_xT%d" % i, [128, KD, TB], F32) for i in range(3 if d == 1 else 2)])
            sq = sb(st, "m_sq", [128, KD, TB], BF16)
            lnv = sb(st, "m_ln", [128, TB], F32)
            tmpr = Rot([sb(st, "m_tmp%d" % i, [128, TB], F32) for i in range(2 if d == 0 else 1)])
            hTr = Rot([sb(st, "m_hT%d" % i, [128, KD, TB], BF16) for i in range(2)])
            qTr = Rot([sb(st, "m_qT%d" % i, [128, KD, TB], BF16) for i in range(2)])
            kTr = Rot([sb(st, "m_kT%d" % i, [128, KD, TB], BF16) for i in range(2)])
            kpr = Rot([sb(st, "m_kp%d" % i, [128, 1024], BF16) for i in range(4)])
            kunr = Rot([sb(st, "m_kun%d" % i, [128, 1024], BF16) for i in range(1)])
            var = Rot([sb(st, "m_va%d" % i, [128, 4, 256], BF16) for i in range(4)])
            onec = sb(st, "m_onec", [128, 2], BF16)
            S.op("dve", lambda e: e.memset(onec[:], 1.0), writes=[onec.r])
            gtr = Rot([sb(st, "m_gt%d" % i, [128, 16], F32) for i in range(4)])
            smr = Rot([sb(st, "m_sm%d" % i, [128, 8, 4], F32) for i in range(4)])
            t4r = Rot([sb(st, "m_t4%d" % i, [128, 2, 4], F32) for i in range(2)])
            hfr = Rot([sb(st, "m_hf%d" % i, [128, D], F32) for i in range(2 if d == 0 else 1)])
            C32 = sb(st, "m_C32", [128, 4, 2, 257], F32)
            Cb = sb(st, "m_Cb", [128, 4, 2, 258], BF16)
            Sdr = Rot([sb(st, "m_Sd%d" % i, [128, 4, 128], BF16) for i in range(2 if d == 0 else 1)])
            if d == 1:
                hsr = Rot([sb(st, "m_hs%d" % i, [128, D], F32) for i in range(2)])
                ogr = Rot([sb(st, "m_og%d" % i, [128, D], F32) for i in range(1)])
                gar = Rot([sb(st, "m_ga%d" % i, [128, D], BF16) for i in range(2)])
                gTr = Rot([sb(st, "m_gT%d" % i, [128, KD, TB], BF16) for i in range(2)])
                junk = sb(st, "m_junk", [128, 256], BF16)
                ssr = Rot([sb(st, "m_ss%d" % i, [128, 4], F32) for i in range(2)])
            pone = Rot([PB[0], PB[1]])
            pfs = PB[2]
            pbs = PB[3]
            pscore = PB[4]
            pnum = (PB[5], PB[6])
            pcu = PB[7]

            def front(si, s0, p, fst):
                gs_ap = lambda k: gsT[:, l, 0, k, si:si + 1]
                sh_ap = lambda k: modT[:, l, 0, k, si:si + 1]
                xt, hT, qT, kT = xTr.next(), hTr.next(), qTr.next(), kTr.next()
                fst.update(xt=xt, hT=hT, qT=qT, kT=kT, ch={})
                S.dma("sp", lambda e, s: e.dma_start(out=xt[:], in_=xT_dram(xa, p, p + TB)).then_inc(s, 16), reads=[xa.r], writes=[xt.r])
                norm_block((sq, lnv, tmpr, pone.next()), lambda k: xt[:, k, :], TB, gs_ap, sh_ap, hT, hT.r, [xt.r])
                yield
                chs = (0, 1) if d == 0 else (1, 0)
                for slot, ch in enumerate(chs):
                    cs = ch * 128
                    gt, sm = gtr.next(), smr.next()
                    fst["ch"][ch] = dict(gt=gt, sm=sm, kp=kpr.next(), va=var.next())
                    g0 = 64 * slot

                    def mmg(e, cs=cs, g0=g0):
                        ins = None
                        for k in range(KD):
                            ins = e.matmul(pfs[:, g0:g0 + 16], hT[:, k, cs:cs + 128], w[:, k, 3072:3088], start=(k == 0), stop=(k == KD - 1))
                        return ins
                    S.op("pe", mmg, reads=[w.r, hT.r], writes=[pfs.r])
                    S.op("dve", lambda e, gt=gt, g0=g0: e.tensor_tensor(gt[:], pfs[:, g0:g0 + 16], bcT[:, 0:16], ALU.add), reads=[pfs.r, bcT.r], writes=[gt.r])
                    S.op("act", lambda e, gt=gt, sm=sm: e.activation(sm[:, 0, :], gt[:, d * 8 + 4:d * 8 + 8], AF.Exp, scale=-1.0), reads=[gt.r], writes=[sm.r])
                    S.op("act", lambda e, sm=sm: e.activation(sm[:, 1, :], sm[:, 0, :], AF.Ln, bias=1.0), reads=[sm.r], writes=[sm.r])
                yield
                for which, dst, c0 in ((0, qT, 0),):
                    for ob in range(KD):
                        pt = pone.next()

                        def mmf(e, ob=ob, pt=pt, c0=c0):
                            ins = None
                            for k in range(KD):
                                ins = e.matmul(pt[:, 0:TB], w[:, k, c0 + ob * 128:c0 + (ob + 1) * 128], hT[:, k, :], start=(k == 0), stop=(k == KD - 1))
                            return ins
                        S.op("pe", mmf, reads=[w.r, hT.r], writes=[pt.r])
                        if which == 0:
                            S.op("act", lambda e, ob=ob, pt=pt: e.activation(qT[:, ob, :], pt[:, 0:TB], AF.Copy, scale=1.0 / 16.0), reads=[pt.r], writes=[qT.r])
                        else:
                            S.op("dve", lambda e, ob=ob, pt=pt: e.tensor_copy(kT[:, ob, :], pt[:, 0:TB]), reads=[pt.r], writes=[kT.r])
                        if ob % 2 == 1:
                            yield
                    if which == 0:
                        for slot, ch in enumerate(chs):
                            c = fst["ch"][ch]
                            gt, sm = c["gt"], c["sm"]
                            g0 = 64 * slot

                            def mmc(e, sm=sm, g0=g0):
                                e.matmul(pfs[:, g0 + 16:g0 + 20], triF[:, d, :], sm[:, 1, :], start=True, stop=True)
                                return e.matmul(pfs[:, g0 + 20:g0 + 24], onesF[:], sm[:, 1, :], start=True, stop=True)
                            S.op("pe", mmc, reads=[sm.r, triF.r, onesF.r], writes=[pfs.r])
                            S.op("act", lambda e, sm=sm, g0=g0: e.activation(sm[:, 0:2, :], pfs[:, g0 + 16:g0 + 24].rearrange("p (a b) -> p a b", a=2), AF.Copy),
                                 reads=[pfs.r, sm.r], writes=[sm.r])
                            S.op("dve", lambda e, gt=gt, sm=sm: e.tensor_tensor(sm[:, 2, :], sm[:, 0, :], gt[:, d * 8:d * 8 + 4], ALU.add),
                                 reads=[sm.r, gt.r], writes=[sm.r])
                            S.op("dve", lambda e, sm=sm: e.tensor_tensor(sm[:, 3, :], sm[:, 2, :], sm[:, 1, :], ALU.subtract),
                                 reads=[sm.r], writes=[sm.r])
                            S.op("act", lambda e, sm=sm: e.activation(sm[:, 6:8, :], sm[:, 0:2, :], AF.Exp, scale=-1.0), reads=[sm.r], writes=[sm.r])
                            S.op("act", lambda e, sm=sm: e.activation(sm[:, 0, :], sm[:, 0, :], AF.Exp), reads=[sm.r], writes=[sm.r])
                            S.op("act", lambda e, sm=sm: e.activation(sm[:, 4, :], sm[:, 2, :], AF.Exp), reads=[sm.r], writes=[sm.r])
                            S.op("act", lambda e, sm=sm: e.activation(sm[:, 5, :], sm[:, 3, :], AF.Exp), reads=[sm.r], writes=[sm.r])
                        yield
                for ch in chs:
                    cs = ch * 128
                    c = fst["ch"][ch]
                    sm, kp, va = c["sm"], c["kp"], c["va"]
                    for half in range(2):
                        pt = pone.next()

                        def mmk(e, pt=pt, half=half, cs=cs):
                            ins = None
                            for k in range(KD):
                                ins = e.matmul(pt[:, :], hT[:, k, cs:cs + 128], w[:, k, 1024 + half * 512:1024 + (half + 1) * 512], start=(k == 0), stop=(k == KD - 1))
                            return ins
                        S.op("pe", mmk, reads=[w.r, hT.r], writes=[pt.r])
                        for j in range(2):
                            h = half * 2 + j
                            if j == 0:
                                S.op("act", lambda e, h=h, pt=pt, kp=kp, sm=sm: e.activation(kp[:, h * 256:(h + 1) * 256], pt[:, 0:256], AF.Copy, scale=sm[:, 5, h:h + 1]),
                                     reads=[pt.r, sm.r], writes=[kp.r])
                            else:
                                S.op("dve", lambda e, h=h, pt=pt, kp=kp, sm=sm: e.tensor_scalar(kp[:, h * 256:(h + 1) * 256], pt[:, 256:512], sm[:, 5, h:h + 1], None, ALU.mult),
                                     reads=[pt.r, sm.r], writes=[kp.r])
                        if half == 0:
                            kun = kunr.next()
                        S.op("act", lambda e, pt=pt, kun=kun, half=half: e.activation(kun[:, half * 512:half * 512 + 256], pt[:, 0:256], AF.Copy), reads=[pt.r], writes=[kun.r])
                        S.op("dve", lambda e, pt=pt, kun=kun, half=half: e.tensor_copy(kun[:, half * 512 + 256:half * 512 + 512], pt[:, 256:512]), reads=[pt.r], writes=[kun.r])
                        yield
                    ptk = pone.next()
                    ptkb = ptk[:].bitcast(BF16)

                    def trk(e, kun=kun, ptkb=ptkb):
                        ins = None
                        for ob in range(KD):
                            ins = e.transpose(ptkb[:, ob * 128:(ob + 1) * 128], kun[:, ob * 128:(ob + 1) * 128], identB[:])
                        return ins
                    S.op("pe", trk, reads=[kun.r, identB.r], writes=[ptk.r])
                    S.op("act", lambda e, cs=cs, ptkb=ptkb: e.activation(kT[:, :, cs:cs + 128], ptkb[:, 0:1024].rearrange("p (k t) -> p k t", k=KD), AF.Copy),
                         reads=[ptk.r], writes=[kT.r])
                    yield
                    for half in range(2):
                        pt = pone.next()

                        def mmv(e, pt=pt, half=half, cs=cs):
                            ins = None
                            for k in range(KD):
                                ins = e.matmul(pt[:, :], hT[:, k, cs:cs + 128], w[:, k, 2048 + half * 512:2048 + (half + 1) * 512], start=(k == 0), stop=(k == KD - 1))
                            return ins
                        S.op("pe", mmv, reads=[w.r, hT.r], writes=[pt.r])
                        if half == 0:
                            S.op("act", lambda e, pt=pt, va=va: e.activation(va[:, 0:2, :], pt[:, :].rearrange("p (h c) -> p h c", h=2), AF.Copy), reads=[pt.r], writes=[va.r])
                        else:
                            S.op("dve", lambda e, pt=pt, va=va: e.tensor_copy(va[:, 2:4, :], pt[:, :].rearrange("p (h c) -> p h c", h=2)), reads=[pt.r], writes=[va.r])
                        yield

            def back(si, p, fst):
                xt, hT, qT, kT = fst["xt"], fst["hT"], fst["qT"], fst["kT"]
                chs = (0, 1) if d == 0 else (1, 0)
                gT = gTr.next() if d == 1 else None
                for ch in chs:
                    cs = ch * 128
                    tok = p + cs
                    c = fst["ch"][ch]
                    sm, kp, va = c["sm"], c["kp"], c["va"]
                    if d == 1:
                        hfl = hfr.next()
                        S.dma("sp", lambda e, s, hfl=hfl, tok=tok: e.dma_start(out=hfl[:], in_=hfw.h.ap()[tok:tok + 128, :]).then_inc(s, 16), reads=[hfw.r], writes=[hfl.r])

                    def mms(e, cs=cs):
                        ins = None
                        for h in range(4):
                            for kb in range(2):
                                ins = e.matmul(pscore[:, h * 128:(h + 1) * 128], kT[:, 2 * h + kb, cs:cs + 128], qT[:, 2 * h + kb, cs:cs + 128], start=(kb == 0), stop=(kb == 1))
                        return ins
                    S.op("pe", mms, reads=[kT.r, qT.r], writes=[pscore.r])
                    Sd = Sdr.next()
                    for h in range(4):
                        S.op("dve", lambda e, h=h, Sd=Sd, sm=sm: e.scalar_tensor_tensor(Sd[:, h, :], pscore[:, h * 128:(h + 1) * 128], sm[:, 4, h:h + 1], triF[:, d, :], ALU.mult, ALU.mult),
                             reads=[pscore.r, sm.r, triF.r], writes=[Sd.r])
                    nold = fmark[0]
                    for _ in range(nold):
                        deferred.pop(0)()
                    fmark[0] = len(deferred)
                    yield

                    def mmn(e, cs=cs, Sd=Sd, va=va):
                        ins = None
                        for h in range(4):
                            pn = pnum[h // 2][:, (h % 2) * 256:(h % 2) * 256 + 256]
                            e.matmul(pn, Sd[:, h, :], va[:, h, :], start=True, stop=False)
                            e.matmul(pn, qT[:, 2 * h, cs:cs + 128], Cb[:, h, 0, 0:256], start=False, stop=False)
                            ins = e.matmul(pn, qT[:, 2 * h + 1, cs:cs + 128], Cb[:, h, 1, 0:256], start=False, stop=True)
                        return ins
                    S.op("pe", mmn, reads=[Sd.r, va.r, qT.r, Cb.r], writes=[pnum[0].r, pnum[1].r])

                    def mmd(e, cs=cs, Sd=Sd):
                        ins = None
                        for h in range(4):
                            e.matmul(pbs[:, h:h + 1], Sd[:, h, :], onec[:, 0:1], start=True, stop=False)
                            e.matmul(pbs[:, h:h + 1], qT[:, 2 * h, cs:cs + 128], Cb[:, h, 0, 256:257], start=False, stop=False)
                            ins = e.matmul(pbs[:, h:h + 1], qT[:, 2 * h + 1, cs:cs + 128], Cb[:, h, 1, 256:257], start=False, stop=True)
                        return ins
                    S.op("pe", mmd, reads=[Sd.r, onec.r, qT.r, Cb.r], writes=[pbs.r])
                    t4 = t4r.next()
                    EB = sm[:, 6, :]
                    S.op("act", lambda e, t4=t4: e.activation(t4[:, 0, :], pbs[:, 0:4], AF.Abs), reads=[pbs.r], writes=[t4.r])
                    S.op("dve", lambda e, t4=t4, sm=sm: e.tensor_tensor(t4[:, 0, :], t4[:, 0, :], sm[:, 0, :], ALU.max), reads=[t4.r, sm.r], writes=[t4.r])
                    S.op("dve", lambda e, t4=t4: e.reciprocal(t4[:, 1, :], t4[:, 0, :]), reads=[t4.r], writes=[t4.r])
                    hout = hfr.next() if d == 0 else hsr.next()
                    for h in range(4):
                        pn = pnum[h // 2][:, (h % 2) * 256:(h % 2) * 256 + 256]
                        if d == 0:
                            if h % 2 == 0:
                                S.op("act", lambda e, h=h, pn=pn, t4=t4, hout=hout: e.activation(hout[:, h * 256:(h + 1) * 256], pn, AF.Copy, scale=t4[:, 1, h:h + 1]),
                                     reads=[pnum[h // 2].r, t4.r], writes=[hout.r])
                            else:
                                S.op("dve", lambda e, h=h, pn=pn, t4=t4, hout=hout: e.tensor_scalar(hout[:, h * 256:(h + 1) * 256], pn, t4[:, 1, h:h + 1], None, ALU.mult),
                                     reads=[pnum[h // 2].r, t4.r], writes=[hout.r])
                        else:
                            S.op("dve", lambda e, h=h, pn=pn, t4=t4, hout=hout, hfl=hfl: e.scalar_tensor_tensor(
                                hout[:, h * 256:(h + 1) * 256], pn, t4[:, 1, h:h + 1], hfl[:, h * 256:(h + 1) * 256], ALU.mult, ALU.add),
                                reads=[pnum[h // 2].r, t4.r, hfl.r], writes=[hout.r])
                    yield
                    if d == 1:
                        og = ogr.next()
                        for half in range(2):
                            pt = pone.next()

                            def mmo(e, pt=pt, half=half, cs=cs):
                                ins = None
                                for k in range(KD):
                                    ins = e.matmul(pt[:, :], hT[:, k, cs:cs + 128], wo_in[:, k, half * 512:(half + 1) * 512], start=(k == 0), stop=(k == KD - 1))
                                return ins
                            S.op("pe", mmo, reads=[wo_in.r, hT.r], writes=[pt.r])
                            S.op("act", lambda e, half=half, pt=pt, og=og: e.activation(og[:, half * 512:(half + 1) * 512], pt[:, :], AF.Sigmoid), reads=[pt.r], writes=[og.r])
                        yield
                    for h in range(4):
                        def mmu(e, h=h, kp=kp, va=va):
                            e.matmul(pcu[:, 0:256], kp[:, h * 256:h * 256 + 128], va[:, h, :], start=True, stop=True)
                            return e.matmul(pcu[:, 256:512], kp[:, h * 256 + 128:h * 256 + 256], va[:, h, :], start=True, stop=True)
                        S.op("pe", mmu, reads=[kp.r, va.r], writes=[pcu.r])
                        S.op("dve", lambda e, h=h, sm=sm: e.scalar_tensor_tensor(C32[:, h, :, 0:256], C32[:, h, :, 0:256], sm[:, 7, h:h + 1],
                                                                                  pcu[:, :].rearrange("p (kb c) -> p kb c", kb=2), ALU.mult, ALU.add),
                             reads=[pcu.r, sm.r, C32.r], writes=[C32.r])

                    def mmnu(e, kp=kp):
                        ins = None
                        for h in range(4):
                            for kb in range(2):
                                ins = e.matmul(pbs[:, 8 + 2 * h + kb:9 + 2 * h + kb], kp[:, h * 256 + kb * 128:h * 256 + (kb + 1) * 128], onec[:, 0:1], start=True, stop=True)
                        return ins
                    S.op("pe", mmnu, reads=[kp.r, onec.r], writes=[pbs.r])
                    S.op("dve", lambda e, sm=sm: e.tensor_tensor(C32[:, :, :, 256], C32[:, :, :, 256], sm[:, 7, :].unsqueeze(2).broadcast_to([128, 4, 2]), ALU.mult),
                         reads=[sm.r, C32.r], writes=[C32.r])
                    S.op("dve", lambda e: e.tensor_tensor(C32[:, :, :, 256], C32[:, :, :, 256], pbs[:, 8:16].rearrange("p (h kb) -> p h kb", kb=2), ALU.add),
                         reads=[pbs.r, C32.r], writes=[C32.r])
                    S.op("act", lambda e: e.activation(Cb[:, :, :, 0:257], C32[:, :, :, :], AF.Copy), reads=[C32.r], writes=[Cb.r])
                    yield
                    if d == 0:
                        S.dma("sp", lambda e, s, hout=hout, tok=tok: e.dma_start(out=hfw.h.ap()[tok:tok + 128, :], in_=hout[:]).then_inc(s, 16), reads=[hout.r], writes=[hfw.r])
                        continue
                    ga, ss = gar.next(), ssr.next()
                    gm = hout
                    for h in range(4):
                        S.op("act", lambda e, h=h, hout=hout, ss=ss: e.activation(junk[:], hout[:, h * 256:(h + 1) * 256], AF.Square, accum_out=ss[:, h:h + 1]),
                             reads=[hout.r], writes=[junk.r, ss.r])
                    S.op("act", lambda e, ss=ss: e.activation(ss[:], ss[:], AF.Ln, bias=EPS, scale=1.0 / 256.0), reads=[ss.r], writes=[ss.r])
                    S.op("act", lambda e, ss=ss: e.activation(ss[:], ss[:], AF.Exp, scale=-0.5), reads=[ss.r], writes=[ss.r])
                    for h in range(4):
                        S.op("dve", lambda e, h=h, hout=hout, ss=ss, gm=gm: e.scalar_tensor_tensor(gm[:, h * 256:(h + 1) * 256], hout[:, h * 256:(h + 1) * 256], ss[:, h:h + 1],
                                                                                                  bcT[:, 16 + h * 256:16 + (h + 1) * 256], ALU.mult, ALU.mult),
                             reads=[hout.r, ss.r, bcT.r], writes=[gm.r])
                    S.op("pool", lambda e, ga=ga, gm=gm, og=og: e.tensor_tensor(ga[:], gm[:], og[:], ALU.mult), reads=[gm.r, og.r], writes=[ga.r])
                    ptb_t = pone.next()
                    ptb = ptb_t[:].bitcast(BF16)

                    def trp(e, ga=ga, ptb=ptb):
                        ins = None
                        for k in range(KD):
                            ins = e.transpose(ptb[:, k * 128:(k + 1) * 128], ga[:, k * 128:(k + 1) * 128], identB[:])
                        return ins
                    def ep_pe(trp=trp, ga=ga, ptb_t=ptb_t, ptb=ptb, cs=cs, gT=gT):
                        S.op("pe", trp, reads=[ga.r, identB.r], writes=[ptb_t.r])
                        S.op("act", lambda e: e.activation(gT[:, :, cs:cs + 128], ptb[:, 0:1024].rearrange("p (k t) -> p k t", k=KD), AF.Copy),
                             reads=[ptb_t.r], writes=[gT.r])
                    deferred.append(ep_pe)
                    yield
                if d == 1:
                    deferred.append(lambda: wout_part(si, p, xt, gT))

            def wout_part(si, p, xt, gT):
                if True:
                    for db in range(KD):
                        pt = pone.next()

                        def mmp(e, db=db, pt=pt, gT=gT):
                            ins = None
                            for k in range(KD):
                                ins = e.matmul(pt[:, 0:TB], wout[:, k, db * 128:(db + 1) * 128], gT[:, k, :], start=(k == 0), stop=(k == KD - 1))
                            return ins
                        S.op("pe", mmp, reads=[wout.r, gT.r], writes=[pt.r])
                        S.op("dve", lambda e, db=db, pt=pt, xt=xt: e.scalar_tensor_tensor(
                            xt[:, db, :], pt[:, 0:TB], modT[:, l, 2, db, si:si + 1], xt[:, db, :], ALU.mult, ALU.add),
                            reads=[pt.r, xt.r, modT.r], writes=[xt.r])
                    S.dma("sp", lambda e, s, xt=xt: e.dma_start(out=xT_dram(xb, p, p + TB), in_=xt[:]).then_inc(s, 16), reads=[xt.r], writes=[xb.r])

            deferred = []
            fmark = [0]

            def drain(g):
                for _ in g:
                    pass

            def interleave(ga_, gb_, ratio):
                a_done = b_done = False
                while not (a_done and b_done):
                    if not a_done:
                        try:
                            next(ga_)
                        except StopIteration:
                            a_done = True
                    for _ in range(ratio):
                        if b_done:
                            break
                        try:
                            next(gb_)
                        except StopIteration:
                            b_done = True

            for si, (s0, SL) in enumerate(zip(starts, seqs)):
                S.op("dve", lambda e: e.memset(C32[:], 0.0), writes=[C32.r])
                S.op("dve", lambda e: e.memset(Cb[:], 0.0), writes=[Cb.r])
                nblk = SL // TB
                order = list(range(nblk)) if d == 0 else list(range(nblk - 1, -1, -1))
                fsts = [dict() for _ in order]
                drain(front(si, s0, s0 + TB * order[0], fsts[0]))
                for j, bi in enumerate(order):
                    bg = back(si, s0 + TB * bi, fsts[j])
                    if j + 1 < len(order):
                        fg = front(si, s0, s0 + TB * order[j + 1], fsts[j + 1])
                        interleave(bg, fg, 3)
                    else:
                        drain(bg)
                    drip(3, C32.r)
                while deferred:
                    deferred.pop(0)()
                fmark[0] = 0
            if d == 1:
                drip(len(castq), C32.r)
            S.flush()

    ctx = dict(final_stage=final_stage, ffn_stage=ffn_stage, xpose_stage=(lambda: None), attn_stage=attn_stage, mlstm_pass=mlstm_pass, xa=xa, xb=xb, hfw=hfw)
    return nc, S, top, ctx


def host_inputs(seq_arrays, c_rows, w):
    NS = len(seq_arrays)
    m = {}
    m["x"] = np.ascontiguousarray(np.concatenate(seq_arrays, axis=0), dtype=np.float32)
    vecs = np.zeros((512, 128), np.float32)
    vecs[0:96] = np.asarray(w["adaln_b"], np.float32).reshape(96, 128)
    vecs[96:128] = np.asarray(w["norm_g"], np.float32).reshape(32, 128)
    vecs[128:136] = np.asarray(w["final_g"], np.float32).reshape(8, 128)
    cc = np.stack([np.asarray(c, np.float32).reshape(8, 128) for c in c_rows], axis=1)
    vecs[136:136 + 8 * NS] = cc.reshape(8 * NS, 128)
    vecs[152:416] = np.asarray(w["ffn_conv_w"], np.float32).reshape(2 * 3 * 44, 128)
    vecs[416:504] = np.asarray(w["ffn_conv_b"], np.float32).reshape(2 * 44, 128)
    m["vecs"] = vecs
    bc = np.concatenate([np.asarray(w["mlstm_b_gate"], np.float32).reshape(16),
                         np.asarray(w["mlstm_head_g"], np.float32).reshape(1024),
                         np.asarray(w["attn_sink"], np.float32).reshape(16)])
    m["bc"] = np.ascontiguousarray(np.broadcast_to(bc[None, :], (128, bc.size)))
    m["adaln_w"] = np.asarray(w["adaln_w"], np.float32)
    m["mlstm_w_in"] = np.asarray(w["mlstm_w_in"], np.float32)[0]
    m["mlstm_w_out"] = np.asarray(w["mlstm_w_out"], np.float32)[0]
    m["attn_w_in"] = np.asarray(w["attn_w_in"], np.float32)[0]
    m["attn_w_out"] = np.asarray(w["attn_w_out"], np.float32)[0]
    m["rel_bias"] = np.asarray(w["rel_bias"], np.float32)
    m["ffn_w_up"] = np.asarray(w["ffn_w_up"], np.float32)
    m["ffn_w_down"] = np.asarray(w["ffn_w_down"], np.float32)
    m.update(_consts())
    return m


def emit_all(ctx):
    ctx["xpose_stage"]()
    ctx["mlstm_pass"](0)
    ctx["mlstm_pass"](1)
    ctx["ffn_stage"](0, ctx["xb"], ctx["xa"], False)
    ctx["attn_stage"](ctx["xa"], ctx["xb"])
    ctx["ffn_stage"](1, ctx["xb"], ctx["xa"], False)
    ctx["final_stage"](ctx["xa"])


_PROG = {}


def _program(seqs):
    key = tuple(seqs)
    if key not in _PROG:
        nc, S, top, ctx = build(list(seqs))
        emit_all(ctx)
        top.close()
        _PROG[key] = nc
    return _PROG[key]


def kernel(x_prompt, x_sample, c_prompt, c_sample, adaln_w, adaln_b, norm_g, mlstm_w_in, mlstm_b_gate,
           mlstm_head_g, mlstm_w_out, attn_w_in, attn_sink, attn_w_out, rel_bias, ffn_w_up, ffn_conv_w,
           ffn_conv_b, ffn_w_down, final_g):
    x_prompt = np.asarray(x_prompt, np.float32)
    x_sample = np.asarray(x_sample, np.float32)
    c_prompt = np.asarray(c_prompt, np.float32)
    c_sample = np.asarray(c_sample, np.float32)
    n = 8
    SP, SS = x_prompt.shape[1], x_sample.shape[1]
    w = dict(adaln_w=adaln_w, adaln_b=adaln_b, norm_g=norm_g, mlstm_w_in=mlstm_w_in, mlstm_b_gate=mlstm_b_gate,
             mlstm_head_g=mlstm_head_g, mlstm_w_out=mlstm_w_out, attn_w_in=attn_w_in, attn_sink=attn_sink,
             attn_w_out=attn_w_out, rel_bias=rel_bias, ffn_w_up=ffn_w_up, ffn_conv_w=ffn_conv_w,
             ffn_conv_b=ffn_conv_b, ffn_w_down=ffn_w_down, final_g=final_g)
    nc = _program([SP, SS])
    base = host_inputs([x_prompt[0], x_sample[0]], [c_prompt[0], c_sample[0]], w)
    in_maps = []
    for i in range(n):
        m = dict(base)
        if i > 0:
            per = host_inputs([x_prompt[i], x_sample[i]], [c_prompt[i], c_sample[i]], w)
            m["x"] = per["x"]
            m["vecs"] = per["vecs"]
        in_maps.append(m)
    res = run_bass_kernel_spmd(nc, in_maps, core_ids=list(range(n)))
    ys = [np.asarray(r["y"], np.float32) for r in res.results]
    y_prompt = np.stack([y[:SP] for y in ys], axis=0)
    y_sample = np.stack([y[SP:SP + SS] for y in ys], axis=0)
    return (y_prompt, y_sample)
```

```python
import contextlib
import os
import numpy as np
import ml_dtypes
import concourse.bass as bass
import concourse.mybir as mybir
from concourse.bass_utils import run_bass_kernel_spmd

F32 = mybir.dt.float32
BF16 = mybir.dt.bfloat16
AF = mybir.ActivationFunctionType
ALU = mybir.AluOpType
AX = mybir.AxisListType

D = 1024
KD = 8
DFF = 2816
NFB = 22
MIN = 4112
EPS = 1e-6
NEG = -30000.0

COMPUTE = ("pe", "act", "dve", "pool")
ENGS = ("sp", "pe", "act", "dve", "pool")
NDMASEM = 24
NBG = 4


class Res:
    __slots__ = ("lw", "rd")

    def __init__(self):
        self.lw = None
        self.rd = []


class Op:
    __slots__ = ("eng", "fn", "deps", "sig", "val", "isdma", "sem", "ndma", "prev", "stage", "bg")


class Sched:
    def __init__(self, nc, stack):
        self.nc = nc
        self.ops = []
        self.stage = 0
        self.dma_rr = 0
        self.dma_last = [None] * NDMASEM
        self.dma_cnt = [0] * NDMASEM
        self.cnt = {e: 0 for e in COMPUTE}
        self.csem = {e: stack.enter_context(nc.semaphore("c_" + e)) for e in COMPUTE}
        self.dsem = [stack.enter_context(nc.semaphore("d_%d" % i)) for i in range(NDMASEM)]
        self.bsem = [stack.enter_context(nc.semaphore("b_%d" % i)) for i in range(NBG)]
        self.bcnt = [0] * NBG
        self.bgres = [Res() for _ in range(NBG)]
        self.waited = {e: {} for e in ENGS}
        self.ninst = {e: 0 for e in ENGS}

    def _deps(self, op, reads, writes):
        deps = []
        for r in reads:
            if r.lw is not None:
                deps.append((r.lw, 0))
        for w in writes:
            if w.lw is not None:
                deps.append((w.lw, 1))
            for q in w.rd:
                deps.append((q, 2))
        seen = set()
        for p, kind in deps:
            if p is op or id(p) in seen or (p.stage != self.stage and not p.bg):
                continue
            if not p.isdma and not op.isdma and p.eng == op.eng:
                if kind != 0 or p.eng == "pe":
                    continue
            seen.add(id(p))
            op.deps.append(p)
            p.sig = True
        for r in reads:
            r.rd.append(op)
        for w in writes:
            w.lw = op
            w.rd = []

    def _mk(self, eng, fn, isdma, ndma):
        o = Op()
        o.eng = eng; o.fn = fn; o.deps = []; o.sig = False; o.val = 0; o.isdma = isdma
        o.sem = None; o.ndma = ndma; o.prev = None; o.stage = self.stage; o.bg = False
        return o

    def op(self, eng, fn, reads=(), writes=()):
        o = self._mk(eng, fn, False, 0)
        self._deps(o, reads, writes)
        self.ops.append(o)
        return o

    def dma(self, eng, fn, reads=(), writes=(), n=1):
        o = self._mk(eng, fn, True, n)
        k = self.dma_rr
        self.dma_rr = (k + 1) % NDMASEM
        o.sem = k
        o.prev = self.dma_last[k]
        self.dma_cnt[k] += 16 * n
        o.val = self.dma_cnt[k]
        self.dma_last[k] = o
        o.sig = True
        self._deps(o, reads, writes)
        self.ops.append(o)
        return o

    def dma_bg(self, eng, fn, group, n=1, reads=()):
        o = self._mk(eng, fn, True, n)
        o.bg = True
        o.sem = group
        self.bcnt[group] += 16 * n
        o.val = self.bcnt[group]
        o.sig = True
        for r in reads:
            p = r.lw
            if p is not None and p.stage == self.stage and p not in o.deps:
                o.deps.append(p)
                p.sig = True
        self.bgres[group].lw = o
        self.ops.append(o)
        return o

    def flush(self):
        nc = self.nc
        last = {}
        for o in self.ops:
            if not o.isdma:
                last[o.eng] = o
        for o in last.values():
            o.sig = True
        for o in self.ops:
            if not o.isdma:
                if o.sig:
                    self.cnt[o.eng] += 1
                o.val = self.cnt[o.eng]
        per = {e: [o for o in self.ops if o.eng == e] for e in ENGS}
        csem, dsem = self.csem, self.dsem

        def run(ename, handle):
            waited = self.waited[ename]

            def wait(sem, key, val):
                if waited.get(key, 0) >= val:
                    return
                handle.wait_ge(sem, val)
                waited[key] = val

            for o in per[ename]:
                for p in o.deps:
                    if p.bg:
                        wait(self.bsem[p.sem], ("b", p.sem), self.bcnt[p.sem])
                    elif p.isdma:
                        wait(dsem[p.sem], ("d", p.sem), p.val)
                    else:
                        wait(csem[p.eng], ("c", p.eng), p.val)
                if o.bg:
                    o.fn(handle, self.bsem[o.sem])
                elif o.isdma:
                    if o.prev is not None:
                        wait(dsem[o.sem], ("d", o.sem), o.prev.val)
                    o.fn(handle, dsem[o.sem])
                else:
                    ins = o.fn(handle)
                    if o.sig:
                        ins.then_inc(csem[o.eng], 1)
                self.ninst[ename] += 1
            for k in range(NDMASEM):
                if self.dma_cnt[k]:
                    wait(dsem[k], ("d", k), self.dma_cnt[k])
            for e in COMPUTE:
                if self.cnt[e]:
                    wait(csem[e], ("c", e), self.cnt[e])

        with nc.Block() as block:
            block.sync(lambda h: run("sp", h))
            block.tensor(lambda h: run("pe", h))
            block.scalar(lambda h: run("act", h))
            block.vector(lambda h: run("dve", h))
            block.gpsimd(lambda h: run("pool", h))
        self.ops = []
        self.stage += 1


class Tl:
    def __init__(self, h, r=None):
        self.h = h
        self.r = r if r is not None else Res()

    def __getitem__(self, k):
        return self.h[k]


class Rot:
    def __init__(self, tiles):
        self.t = tiles
        self.i = 0

    def next(self):
        t = self.t[self.i % len(self.t)]
        self.i += 1
        return t


def _t5_bucket(rel):
    nb = 16
    max_exact = 8
    ret = np.where(rel > 0, nb, 0)
    n = np.abs(rel)
    nf = np.maximum(n, 1).astype(np.float32)
    large = max_exact + (np.log(nf / np.float32(max_exact)) / np.float32(np.log(128 / max_exact))
                         * np.float32(nb - max_exact)).astype(np.int32)
    large = np.minimum(large, nb - 1)
    return ret + np.where(n < max_exact, n, large)


def _consts():
    c = {}
    c["ident"] = np.eye(128, dtype=np.float32)
    s = np.arange(128)[:, None]
    t = np.arange(128)[None, :]
    c["tri"] = np.stack([(s <= t), (s >= t)]).astype(np.float32)
    c["antiI"] = np.ascontiguousarray(np.eye(128, dtype=np.float32)[::-1])
    oh = np.zeros((33, 512), np.float32)
    for rp in range(511):
        rel = rp - 255
        if abs(rel) <= 128:
            oh[int(_t5_bucket(np.array(rel))), rp] = 1.0
        else:
            oh[32, rp] = 1.0
    oh[32, 511] = 1.0
    c["oh"] = oh
    return c


def build(seqs, debug=False):
    NT = sum(seqs)
    starts = [sum(seqs[:i]) for i in range(len(seqs))]
    NS = len(seqs)
    nc = bass.Bass("TRN2", target_bir_lowering=False)
    dt_in = {}

    def din(name, shape):
        dt_in[name] = nc.dram_tensor(name, list(shape), F32, kind="ExternalInput")
        return dt_in[name]

    x_in = din("x", [NT, D])
    vecs_in = din("vecs", [512, 128])
    bc_in = din("bc", [128, 16 + 1024 + 16])
    adaln_w = din("adaln_w", [2, D, 6 * D])
    m_w_in = din("mlstm_w_in", [D, MIN])
    m_w_out = din("mlstm_w_out", [D, D])
    a_w_in = din("attn_w_in", [D, 1536])
    a_w_out = din("attn_w_out", [D, D])
    rel_bias = din("rel_bias", [32, 16])
    f_w_up = din("ffn_w_up", [2, D, 2 * DFF])
    f_w_dn = din("ffn_w_down", [2, DFF, D])
    c_ident = din("ident", [128, 128])
    c_tri = din("tri", [2, 128, 128])
    c_oh = din("oh", [33, 512])
    c_anti = din("antiI", [128, 128])
    y_out = nc.dram_tensor("y", [NT, D], F32, kind="ExternalOutput")
    skind = "ExternalOutput" if debug else "Internal"

    def dscr(name, shape, dt, kind="Internal"):
        return Tl(nc.dram_tensor(name, list(shape), dt, kind=kind))

    xa = dscr("xa", [KD, 128, NT], F32, skind)
    xb = dscr("xb", [KD, 128, NT], F32, skind)
    hfw = dscr("hfw", [NT, D], F32, skind)
    fd = dscr("fd", [16, 512], F32)
    wb_min = dscr("wb_min", [D, MIN], BF16)
    wb_mout = dscr("wb_mout", [D, D], BF16)
    wb_ain = dscr("wb_ain", [D, 1536], BF16)
    wb_aout = dscr("wb_aout", [D, D], BF16)
    wb_up = [dscr("wb_up%d" % l, [NFB, 128, KD, 256], BF16) for l in range(2)]
    wb_dn = [dscr("wb_dn%d" % l, [KD, 128, NFB, 128], BF16) for l in range(2)]

    top = contextlib.ExitStack()
    S = Sched(nc, top)

    uid = [0]

    def sb(st, name, shape, dt):
        uid[0] += 1
        return Tl(st.enter_context(nc.sbuf_tensor("%s_%d" % (name, uid[0]), list(shape), dt)))

    def dbg(name, ap, res, dt=F32):
        if not debug:
            return
        uid[0] += 1
        t = nc.dram_tensor("dbg_%s_%d" % (name, uid[0]), list(ap.shape), dt, kind="ExternalOutput")
        S.dma("sp", lambda e, s: e.dma_start(out=t.ap(), in_=ap).then_inc(s, 16), reads=[res])

    PB = [Tl(top.enter_context(nc.psum_tensor("pb%d" % i, [128, 512], F32))) for i in range(8)]

    def xT_dram(t, c0, c1):
        return t.h.ap()[:, :, c0:c1].rearrange("k p t -> p k t")

    identF = sb(top, "identF", [128, 128], F32)
    identB = sb(top, "identB", [128, 128], BF16)
    onesB = sb(top, "onesB", [128, 128], BF16)
    onesF = sb(top, "onesF", [128, 128], F32)
    triF = sb(top, "triF", [128, 2, 128], F32)
    vecsT = sb(top, "vecsT", [128, 512], F32)
    bcT = sb(top, "bcT", [128, 16 + 1024 + 16], F32)
    modT = sb(top, "modT", [128, 2, 6, KD, NS], F32)
    gsT = sb(top, "gsT", [128, 2, 2, KD, NS], F32)
    esink = sb(top, "esink", [128, 16], F32)
    zcol = sb(top, "zcol", [128, 1], F32)

    V_ADB, V_NG, V_FG, V_C, V_CW, V_CB = 0, 96, 128, 136, 152, 416

    def v_adb(l, j):
        return vecsT[:, V_ADB + (l * 6 + j) * 8: V_ADB + (l * 6 + j) * 8 + 8]

    def v_cw(l, t, f):
        c = V_CW + (l * 3 + t) * 44 + f
        return vecsT[:, c:c + 1]

    def v_cb(l, f):
        c = V_CB + l * 44 + f
        return vecsT[:, c:c + 1]

    def xpose_ops(st):
        xin_t = Rot([sb(st, "t_xin%d" % i, [128, D], F32) for i in range(3)])
        xo_t = Rot([sb(st, "t_xo%d" % i, [128, KD, 512], F32) for i in range(2)])
        bk = Rot([(PB[4], PB[5]), (PB[6], PB[7])])
        for blk in range(NT // 512):
            xo = xo_t.next()
            for cc in range(4):
                c = blk * 4 + cc
                xi = xin_t.next()
                S.dma("sp", lambda e, s, xi=xi, c=c: e.dma_start(out=xi[:], in_=x_in.ap()[c * 128:(c + 1) * 128, :]).then_inc(s, 16), writes=[xi.r])
                b0, b1 = bk.next()

                def tr(e, xi=xi, b0=b0, b1=b1):
                    ins = None
                    for k in range(KD):
                        ins = e.transpose((b0 if k < 4 else b1)[:, (k % 4) * 128:(k % 4) * 128 + 128], xi[:, k * 128:(k + 1) * 128], identF[:])
                    return ins
                S.op("pe", tr, reads=[xi.r, identF.r], writes=[b0.r, b1.r])
                S.op("act", lambda e, xo=xo, b0=b0, cc=cc: e.activation(xo[:, 0:4, cc * 128:(cc + 1) * 128], b0[:].rearrange("p (k t) -> p k t", k=4), AF.Copy), reads=[b0.r], writes=[xo.r])
                S.op("dve", lambda e, xo=xo, b1=b1, cc=cc: e.tensor_copy(xo[:, 4:8, cc * 128:(cc + 1) * 128], b1[:].rearrange("p (k t) -> p k t", k=4)), reads=[b1.r], writes=[xo.r])
            S.dma("sp", lambda e, s, xo=xo, blk=blk: e.dma_start(out=xT_dram(xa, blk * 512, (blk + 1) * 512), in_=xo[:]).then_inc(s, 16), reads=[xo.r], writes=[xa.r])

    castq = []

    def drip(k, res):
        for _ in range(k):
            if not castq:
                return
            grp, fn = castq.pop(0)
            S.dma_bg("pool", fn, grp, reads=[res])

    with contextlib.ExitStack() as st:
        S.dma("sp", lambda e, s: e.dma_start(out=identF[:], in_=c_ident.ap()).then_inc(s, 16), writes=[identF.r])
        S.dma("sp", lambda e, s: e.dma_start(out=triF[:], in_=c_tri.ap().rearrange("d s t -> s d t")).then_inc(s, 16),
              writes=[triF.r])
        S.dma("sp", lambda e, s: e.dma_start(out=bcT[:], in_=bc_in.ap()).then_inc(s, 16), writes=[bcT.r])
        S.op("act", lambda e: e.activation(identB[:], identF[:], AF.Copy), reads=[identF.r], writes=[identB.r])
        S.op("dve", lambda e: e.memset(onesB[:], 1.0), writes=[onesB.r])
        S.op("dve", lambda e: e.memset(onesF[:], 1.0), writes=[onesF.r])
        S.op("dve", lambda e: e.memset(zcol[:], 0.0), writes=[zcol.r])
        wres = Res()

        def cast_rows(dst, src, rows, nsplit):
            step = rows // nsplit
            for i in range(nsplit):
                S.dma("pool", lambda e, s, i=i: e.dma_start(out=dst.h.ap()[i * step:(i + 1) * step, :],
                                                            in_=src.ap()[i * step:(i + 1) * step, :]).then_inc(s, 16),
                      writes=[wres])


        vtmp = sb(st, "vtmp", [128, 4, 128], F32)
        S.dma("sp", lambda e, s: e.dma_start(out=vtmp[:], in_=vecs_in.ap().rearrange("(a r) c -> r a c", r=128)).then_inc(s, 16),
              writes=[vtmp.r])
        for a in range(4):
            S.op("pe", lambda e, a=a: e.transpose(PB[0][:, a * 128:(a + 1) * 128], vtmp[:, a, :], identF[:]),
                 reads=[vtmp.r, identF.r], writes=[PB[0].r])
        S.op("dve", lambda e: e.tensor_copy(vecsT[:], PB[0][:]), reads=[PB[0].r], writes=[vecsT.r])
        csT = sb(st, "csT", [128, KD, NS], F32)
        S.op("act", lambda e: e.activation(csT[:], vecsT[:, V_C:V_C + KD * NS].rearrange("p (k s) -> p k s", s=NS), AF.Silu),
             reads=[vecsT.r], writes=[csT.r])
        awt = Rot([sb(st, "awt%d" % i, [128, KD, 1024], F32) for i in range(2)])
        for l in range(2):
            for j in range(6):
                w = awt.next()
                S.dma("sp", lambda e, s, w=w, l=l, j=j: e.dma_start(
                    out=w[:], in_=adaln_w.ap()[l].rearrange("(k p) n -> p k n", p=128)[:, :, j * 1024:(j + 1) * 1024]
                ).then_inc(s, 16), writes=[w.r])
                pm = PB[1 + (l * 6 + j) % 2]

                def mm(e, w=w, pm=pm):
                    ins = None
                    for m in range(KD):
                        for k in range(KD):
                            ins = e.matmul(pm[:, m * NS:(m + 1) * NS], w[:, k, m * 128:(m + 1) * 128], csT[:, k, :],
                                           start=(k == 0), stop=(k == KD - 1))
                    return ins
                S.op("pe", mm, reads=[w.r, csT.r], writes=[pm.r])
                S.op("dve", lambda e, pm=pm, l=l, j=j: e.tensor_tensor(
                    modT[:, l, j, :, :], pm[:, 0:KD * NS].rearrange("p (m s) -> p m s", s=NS),
                    v_adb(l, j).unsqueeze(2).broadcast_to([128, KD, NS]), ALU.add),
                    reads=[pm.r, vecsT.r], writes=[modT.r])
        for l in range(2):
            for i in range(2):
                ng = vecsT[:, V_NG + (l * 2 + i) * 8: V_NG + (l * 2 + i) * 8 + 8]
                S.op("dve", lambda e, l=l, i=i, ng=ng: e.scalar_tensor_tensor(
                    gsT[:, l, i, :, :], modT[:, l, 1 + 3 * i, :, :], 1.0, ng.unsqueeze(2).broadcast_to([128, KD, NS]),
                    ALU.add, ALU.mult), reads=[modT.r, vecsT.r], writes=[gsT.r])
        S.op("act", lambda e: e.activation(esink[:], bcT[:, 1040:1056], AF.Exp), reads=[bcT.r], writes=[esink.r])
        rba = sb(st, "rba", [33, 16], F32)
        oht = sb(st, "oht", [33, 512], F32)
        fds = sb(st, "fds", [16, 512], F32)
        S.op("dve", lambda e: e.memset(rba[32:33, :], NEG), writes=[rba.r])
        S.dma("sp", lambda e, s: e.dma_start(out=rba[0:32, :], in_=rel_bias.ap()).then_inc(s, 16), reads=[rba.r], writes=[rba.r])
        S.dma("sp", lambda e, s: e.dma_start(out=oht[:], in_=c_oh.ap()).then_inc(s, 16), writes=[oht.r])
        S.op("pe", lambda e: e.matmul(PB[3][0:16, :], rba[:], oht[:], start=True, stop=True), reads=[rba.r, oht.r], writes=[PB[3].r])
        S.op("act", lambda e: e.activation(fds[:], PB[3][0:16, :], AF.Copy), reads=[PB[3].r], writes=[fds.r])
        S.dma("sp", lambda e, s: e.dma_start(out=fd.h.ap(), in_=fds[:]).then_inc(s, 16), reads=[fds.r], writes=[fd.r])
        def cast_rows_bg(dst, src, rows, nsplit, grp):
            step = rows // nsplit
            for i in range(nsplit):
                castq.append((grp, lambda e, s, i=i: e.dma_start(out=dst.h.ap()[i * step:(i + 1) * step, :],
                                                                 in_=src.ap()[i * step:(i + 1) * step, :]).then_inc(s, 16)))

        cast_rows_bg(wb_min, m_w_in, D, 8, 3)
        cast_rows_bg(wb_mout, m_w_out, D, 2, 3)
        for l in range(2):
            grp = 0 if l == 0 else 2
            for fb in range(NFB):
                for two in range(2):
                    castq.append((grp, lambda e, s, l=l, fb=fb, two=two: e.dma_start(
                        out=wb_up[l].h.ap()[fb][:, :, two * 128:(two + 1) * 128],
                        in_=f_w_up.ap()[l].rearrange("(k p) n -> p k n", p=128)[:, :, two * DFF + fb * 128:two * DFF + (fb + 1) * 128]
                    ).then_inc(s, 16)))
            for db in range(KD):
                castq.append((grp, lambda e, s, l=l, db=db: e.dma_start(
                    out=wb_dn[l].h.ap()[db],
                    in_=f_w_dn.ap()[l].rearrange("(f p) n -> p f n", p=128)[:, :, db * 128:(db + 1) * 128]
                ).then_inc(s, 16)))
            if l == 0:
                cast_rows_bg(wb_ain, a_w_in, D, 4, 1)
                cast_rows_bg(wb_aout, a_w_out, D, 2, 1)
        xpose_ops(st)
        if debug:
            dbg_mod = nc.dram_tensor("dbg_mod", [128, 2 * 6 * KD * NS], F32, kind="ExternalOutput")
            dbg_gs = nc.dram_tensor("dbg_gs", [128, 2 * 2 * KD * NS], F32, kind="ExternalOutput")
            dbg_vec = nc.dram_tensor("dbg_vec", [128, 512], F32, kind="ExternalOutput")
            S.dma("sp", lambda e, s: e.dma_start(out=dbg_mod.ap(), in_=modT[:].rearrange("p a b c d -> p (a b c d)")).then_inc(s, 16), reads=[modT.r])
            S.dma("sp", lambda e, s: e.dma_start(out=dbg_gs.ap(), in_=gsT[:].rearrange("p a b c d -> p (a b c d)")).then_inc(s, 16), reads=[gsT.r])
            S.dma("sp", lambda e, s: e.dma_start(out=dbg_vec.ap(), in_=vecsT[:]).then_inc(s, 16), reads=[vecsT.r])
        S.flush()

    def norm_block(st_tiles, xsrc, n, gs_ap, sh_ap, hout, hres, extra_reads, out_cols0=0):
        sq, lnv, tmpr, pss = st_tiles
        for k in range(KD):
            S.op("act", lambda e, k=k: e.activation(sq[:, k, 0:n], xsrc(k), AF.Square), reads=extra_reads, writes=[sq.r])

        def mm(e):
            ins = None
            for k in range(KD):
                ins = e.matmul(pss[:, 0:n], onesB[:], sq[:, k, 0:n], start=(k == 0), stop=(k == KD - 1))
            return ins
        S.op("pe", mm, reads=[sq.r, onesB.r], writes=[pss.r])
        S.op("act", lambda e: e.activation(lnv[:, 0:n], pss[:, 0:n], AF.Ln, bias=EPS, scale=1.0 / D), reads=[pss.r], writes=[lnv.r])
        S.op("act", lambda e: e.activation(lnv[:, 0:n], lnv[:, 0:n], AF.Exp, scale=-0.5), reads=[lnv.r], writes=[lnv.r])
        for k in range(KD):
            t = tmpr.next()
            S.op("dve", lambda e, k=k, t=t: e.tensor_tensor(t[:, 0:n], xsrc(k), lnv[:, 0:n], ALU.mult),
                 reads=list(extra_reads) + [lnv.r], writes=[t.r])
            if sh_ap is not None:
                S.op("act", lambda e, k=k, t=t: e.activation(hout[:, k, out_cols0:out_cols0 + n], t[:, 0:n], AF.Identity,
                                                              bias=sh_ap(k), scale=gs_ap(k)),
                     reads=[t.r, modT.r, gsT.r], writes=[hres])
            else:
                S.op("act", lambda e, k=k, t=t: e.activation(hout[:, k, out_cols0:out_cols0 + n], t[:, 0:n], AF.Copy,
                                                              scale=gs_ap(k)),
                     reads=[t.r, vecsT.r], writes=[hres])

    def ffn_stage(l, xin, xout, final):
        with contextlib.ExitStack() as st:
            xblk = Rot([sb(st, "f_x%d" % i, [128, KD, 513], F32) for i in range(2)])
            sq = sb(st, "f_sq", [128, KD, 512], BF16)
            lnv = sb(st, "f_ln", [128, 512], F32)
            tmpr = Rot([sb(st, "f_tmp%d" % i, [128, 512], F32) for i in range(2)])
            hTr = Rot([sb(st, "f_hT%d" % i, [128, KD, 512], BF16) for i in range(2)])
            wup = Rot([sb(st, "f_wu%d" % i, [128, KD, 256], BF16) for i in range(3)])
            ug = Rot([sb(st, "f_ug%d" % i, [128, 514], F32) for i in range(2)])
            uv = Rot([sb(st, "f_uv%d" % i, [128, 514], F32) for i in range(2)])
            tg = Rot([sb(st, "f_tg%d" % i, [128, 512], F32) for i in range(3)])
            tv = Rot([sb(st, "f_tv%d" % i, [128, 512], F32) for i in range(3)])
            sg = Rot([sb(st, "f_sg%d" % i, [128, 512], F32) for i in range(2)])
            actr = Rot([sb(st, "f_act%d" % i, [128, NFB, 512], BF16) for i in range(2)])
            wdn = Rot([sb(st, "f_wd%d" % i, [128, NFB, 128], BF16) for i in range(3)])
            carry = sb(st, "f_carry", [128, 2 * NFB, 2], F32)
            Er = Rot([sb(st, "f_E%d" % i, [128, 2 * NFB, 4], F32) for i in range(2)])
            Ez = sb(st, "f_Ez", [128, 2 * NFB, 4], F32)
            S.op("pool", lambda e: e.memset(Ez[:], 0.0), writes=[Ez.r])
            epr = Rot([sb(st, "f_ep%d" % i, [128, 2 * NFB, 2], F32) for i in range(2)])
            etm = Rot([sb(st, "f_et%d" % i, [128, 2 * NFB], F32) for i in range(2)])
            esr = Rot([sb(st, "f_es%d" % i, [128, NFB, 2], F32) for i in range(2)])
            lastE = [None]
            if final:
                yn = sb(st, "f_yn", [128, KD, 512], F32)
                ytok = Rot([sb(st, "f_yt%d" % i, [128, D], F32) for i in range(2)])
            pg = Rot([PB[0], PB[1]])
            pv = Rot([PB[2], PB[3]])
            py = Rot([PB[4], PB[5]])
            pss = PB[6]
            tailq = []
            storeq = []
            finq = []

            def flush_tail():
                while tailq:
                    tailq.pop(0)()

            def flush_stores():
                while storeq:
                    storeq.pop(0)()

            class Blk:
                pass

            def load_norm(si, s0, nb, i):
                b = Blk()
                b.si, b.i = si, i
                b.tail = (i == nb)
                b.n = 1 if b.tail else 512
                b.p = s0 + 512 * i
                b.c0 = 1 if i == 0 else 0
                b.xt = xblk.next()
                b.hT = None
                b.act = actr.next()
                b.Eprev = lastE[0] if i > 0 else Ez
                b.E = None if b.tail else Er.next()
                lastE[0] = b.E
                xt, c0, p, n = b.xt, b.c0, b.p, b.n
                hi = 1 if b.tail else n + 1
                S.dma("sp", lambda e, s: e.dma_start(out=xt[:, :, c0:hi], in_=xT_dram(xin, p - 1 + c0, p - 1 + hi), allow_slow_non_contiguous=True).then_inc(s, 16),
                      reads=[xin.r], writes=[xt.r])
                if not b.tail:
                    b.hT = hTr.next()
                    norm_block((sq, lnv, tmpr, pss), lambda k: xt[:, k, 1:1 + n], n, lambda k: gsT[:, l, 1, k, si:si + 1],
                               lambda k: modT[:, l, 3, k, si:si + 1], b.hT, b.hT.r, [xt.r])
                return b

            def up(b, fbs, hook=None):
                n, hT, act = b.n, b.hT, b.act
                if b.tail:
                    flush_tail()
                    return
                E = b.E
                LO = 2
                for fb in fbs:
                    g_t, v_t = ug.next(), uv.next()
                    a_t, b_t = tg.next(), tv.next()
                    chains = ((g_t, a_t, fb), (v_t, b_t, NFB + fb))
                    wt = wup.next()
                    S.dma("sp", lambda e, s, wt=wt, fb=fb: e.dma_start(out=wt[:], in_=wb_up[l].h.ap()[fb]).then_inc(s, 16), reads=[S.bgres[0 if l == 0 else 2]], writes=[wt.r])
                    for (pt, ut, ot, off, f) in ((pg.next(), g_t, a_t, 0, fb), (pv.next(), v_t, b_t, 128, NFB + fb)):
                        def mm(e, wt=wt, pt=pt, off=off):
                            ins = None
                            for k in range(KD):
                                ins = e.matmul(pt[:, 0:n], wt[:, k, off:off + 128], hT[:, k, 0:n], start=(k == 0), stop=(k == KD - 1))
                            return ins
                        S.op("pe", mm, reads=[wt.r, hT.r], writes=[pt.r])
                        S.op("act", lambda e, pt=pt, ut=ut: e.activation(ut[:, 2:2 + n], pt[:, 0:n], AF.Copy), reads=[pt.r], writes=[ut.r])
                        L1 = max(LO, 1)
                        S.op("act", lambda e, pt=pt, ot=ot, f=f, L1=L1: e.activation(ot[:, L1:n], pt[:, L1 - 1:n - 1], AF.Identity, bias=v_cb(l, f), scale=v_cw(l, 1, f)),
                             reads=[pt.r, vecsT.r], writes=[ot.r])
                    for (ut, ot, f) in chains:
                        S.op("pool", lambda e, ut=ut, f=f: e.tensor_copy(E[:, f, 0:2], ut[:, 2:4]), reads=[ut.r], writes=[E.r])
                        S.op("pool", lambda e, ut=ut, f=f: e.tensor_copy(E[:, f, 2:4], ut[:, n:n + 2]), reads=[ut.r], writes=[E.r])
                    flush_tail()
                    for (ut, ot, f) in chains:
                        S.op("dve", lambda e, ut=ut, ot=ot, f=f: e.scalar_tensor_tensor(ot[:, LO:n], ut[:, LO:n], v_cw(l, 0, f), ot[:, LO:n], ALU.mult, ALU.add),
                             reads=[ut.r, ot.r, vecsT.r], writes=[ot.r])
                    for (ut, ot, f) in chains:
                        S.op("dve", lambda e, ut=ut, ot=ot, f=f: e.scalar_tensor_tensor(ot[:, LO:n], ut[:, LO + 2:2 + n], v_cw(l, 2, f), ot[:, LO:n], ALU.mult, ALU.add),
                             reads=[ut.r, ot.r, vecsT.r], writes=[ot.r])

                    def tail_ops(a_t=a_t, b_t=b_t, fb=fb):
                        s_t = sg.next()
                        S.op("act", lambda e: e.activation(s_t[:, LO:n], a_t[:, LO:n], AF.Silu), reads=[a_t.r], writes=[s_t.r])
                        S.op("pool", lambda e: e.tensor_tensor(act[:, fb, LO:n], s_t[:, LO:n], b_t[:, LO:n], ALU.mult), reads=[s_t.r, b_t.r], writes=[act.r])
                    tailq.append(tail_ops)
                    if fb == 12:
                        flush_stores()
                    if fb == 14 and hook is not None:
                        hook()
                    if fb == 8:
                        while finq:
                            finq.pop(0)()

            def edge(b):
                act = b.act
                C = b.Eprev
                Nn = b.E if not b.tail else Ez
                ep, tm, se = epr.next(), etm.next(), esr.next()
                W = [vecsT[:, V_CW + (l * 3 + t) * 44: V_CW + (l * 3 + t) * 44 + 44] for t in range(3)]
                Bv = vecsT[:, V_CB + l * 44: V_CB + l * 44 + 44]
                for col, (x0, x1, x2) in enumerate(((C[:, :, 2], C[:, :, 3], Nn[:, :, 0]), (C[:, :, 3], Nn[:, :, 0], Nn[:, :, 1]))):
                    o = ep[:, :, col]
                    rd = [C.r, Nn.r, vecsT.r, ep.r, tm.r]
                    S.op("pool", lambda e, o=o, x0=x0: e.tensor_tensor(o, W[0], x0, ALU.mult), reads=rd, writes=[ep.r])
                    S.op("pool", lambda e, x1=x1: e.tensor_tensor(tm[:], W[1], x1, ALU.mult), reads=rd, writes=[tm.r])
                    S.op("pool", lambda e, o=o: e.tensor_tensor(o, o, tm[:], ALU.add), reads=rd, writes=[ep.r])
                    S.op("pool", lambda e, x2=x2: e.tensor_tensor(tm[:], W[2], x2, ALU.mult), reads=rd, writes=[tm.r])
                    S.op("pool", lambda e, o=o: e.tensor_tensor(o, o, tm[:], ALU.add), reads=rd, writes=[ep.r])
                    S.op("pool", lambda e, o=o: e.tensor_tensor(o, o, Bv, ALU.add), reads=rd, writes=[ep.r])
                S.op("act", lambda e: e.activation(se[:], ep[:, 0:NFB, :], AF.Silu), reads=[ep.r], writes=[se.r])
                S.op("pool", lambda e: e.tensor_tensor(act[:, :, 0:2], se[:], ep[:, NFB:2 * NFB, :], ALU.mult), reads=[se.r, ep.r], writes=[act.r])

            def down(b):
                n, xt, act, si, c0, p = b.n, b.xt, b.act, b.si, b.c0, b.p
                if debug and l == 0 and b.si == 1 and b.i == 0:
                    dbg("act", act[:], act.r, BF16)
                for db in range(KD):
                    wd = wdn.next()
                    S.dma("sp", lambda e, s, wd=wd, db=db: e.dma_start(out=wd[:], in_=wb_dn[l].h.ap()[db]).then_inc(s, 16), reads=[S.bgres[0 if l == 0 else 2]], writes=[wd.r])
                    pt = py.next()

                    def mm(e, wd=wd, pt=pt):
                        ins = None
                        for f in range(NFB):
                            ins = e.matmul(pt[:, 0:n], wd[:, f, :], act[:, f, 0:n], start=(f == 0), stop=(f == NFB - 1))
                        return ins
                    S.op("pe", mm, reads=[wd.r, act.r], writes=[pt.r])
                    S.op("dve", lambda e, pt=pt, db=db: e.scalar_tensor_tensor(
                        xt[:, db, 0:n], pt[:, 0:n], modT[:, l, 5, db, si:si + 1], xt[:, db, 0:n], ALU.mult, ALU.add),
                        reads=[pt.r, xt.r, modT.r], writes=[xt.r])
                tok0 = p - 1 + c0
                if not final:
                    storeq.append(lambda: S.dma("sp", lambda e, s: e.dma_start(out=xT_dram(xout, tok0, tok0 + n - c0), in_=xt[:, :, c0:n],
                                                                               allow_slow_non_contiguous=True).then_inc(s, 16),
                                                reads=[xt.r], writes=[xout.r]))
                    return
                finq.append(lambda: final_part(n, xt, c0, p))

            def final_part(n, xt, c0, p):
                norm_block((sq, lnv, tmpr, pss), lambda k: xt[:, k, 0:n], n, lambda k: vecsT[:, V_FG + k:V_FG + k + 1], None, yn, yn.r, [xt.r])
                for g0 in range(0, n, 128):
                    wdt = min(128, n - g0)
                    banks = [py.next(), py.next()]

                    def tr(e, g0=g0, wdt=wdt, banks=banks):
                        ins = None
                        for k in range(KD):
                            ins = e.transpose(banks[k // 4][0:wdt, (k % 4) * 128:(k % 4) * 128 + 128], yn[:, k, g0:g0 + wdt], identF[:])
                        return ins
                    S.op("pe", tr, reads=[yn.r, identF.r], writes=[banks[0].r, banks[1].r])
                    yt = ytok.next()
                    S.op("act", lambda e, yt=yt, banks=banks, wdt=wdt: e.activation(yt[0:wdt, 0:512], banks[0][0:wdt, :], AF.Copy),
                         reads=[banks[0].r], writes=[yt.r])
                    S.op("dve", lambda e, yt=yt, banks=banks, wdt=wdt: e.tensor_copy(yt[0:wdt, 512:1024], banks[1][0:wdt, :]),
                         reads=[banks[1].r], writes=[yt.r])
                    r0 = c0 if g0 == 0 else 0
                    t0 = p - 1 + g0 + r0
                    if wdt - r0 > 0:
                        S.dma("sp", lambda e, s, yt=yt, r0=r0, wdt=wdt, t0=t0: e.dma_start(
                            out=y_out.ap()[t0:t0 + wdt - r0, :], in_=yt[r0:wdt, :]).then_inc(s, 16), reads=[yt.r])

            for si, (s0, SL) in enumerate(zip(starts, seqs)):
                nb = SL // 512
                flush_tail()
                S.op("pool", lambda e: e.memset(carry[:], 0.0), writes=[carry.r])
                NPRE = 6
                cur = load_norm(si, s0, nb, 0)
                for i in range(nb + 1):
                    holder = []
                    hook = (lambda i=i, holder=holder: holder.append(load_norm(si, s0, nb, i + 1))) if i < nb else None
                    up(cur, range(NFB) if i == 0 else range(NPRE, NFB), hook)
                    edge(cur)
                    if i < nb:
                        nxt = holder[0]
                        up(nxt, range(0, NPRE))
                        down(cur)
                        cur = nxt
                flush_tail()
                down(cur)
                flush_stores()
                while finq:
                    finq.pop(0)()
            S.flush()

    def final_stage(xin):
        with contextlib.ExitStack() as st:
            xr = Rot([sb(st, "o_x%d" % i, [128, KD, 512], F32) for i in range(3)])
            sq = sb(st, "o_sq", [128, KD, 512], BF16)
            lnv = sb(st, "o_ln", [128, 512], F32)
            tmpr = Rot([sb(st, "o_tmp%d" % i, [128, 512], F32) for i in range(2)])
            ynr = Rot([sb(st, "o_yn%d" % i, [128, KD, 512], F32) for i in range(2)])
            ytok = Rot([sb(st, "o_yt%d" % i, [128, D], F32) for i in range(4)])
            bankr = Rot([(PB[0], PB[1]), (PB[2], PB[3]), (PB[4], PB[5])])
            pss = PB[6]

            def load_norm(blk):
                xt, yn = xr.next(), ynr.next()
                S.dma("sp", lambda e, s: e.dma_start(out=xt[:], in_=xT_dram(xin, blk * 512, (blk + 1) * 512)).then_inc(s, 16),
                      reads=[xin.r], writes=[xt.r])
                norm_block((sq, lnv, tmpr, pss), lambda k: xt[:, k, :], 512, lambda k: vecsT[:, V_FG + k:V_FG + k + 1], None, yn, yn.r, [xt.r])
                return yn

            def emit_out(blk, yn):
                for g in range(4):
                    banks = bankr.next()

                    def tr(e, g=g, banks=banks):
                        ins = None
                        for k in range(KD):
                            ins = e.transpose(banks[k // 4][:, (k % 4) * 128:(k % 4) * 128 + 128], yn[:, k, g * 128:(g + 1) * 128], identF[:])
                        return ins
                    S.op("pe", tr, reads=[yn.r, identF.r], writes=[banks[0].r, banks[1].r])
                    yt = ytok.next()
                    S.op("act", lambda e, yt=yt, banks=banks: e.activation(yt[:, 0:512], banks[0][:, :], AF.Copy), reads=[banks[0].r], writes=[yt.r])
                    S.op("dve", lambda e, yt=yt, banks=banks: e.tensor_copy(yt[:, 512:1024], banks[1][:, :]), reads=[banks[1].r], writes=[yt.r])
                    t0 = blk * 512 + g * 128
                    S.dma("sp", lambda e, s, yt=yt, t0=t0: e.dma_start(out=y_out.ap()[t0:t0 + 128, :], in_=yt[:]).then_inc(s, 16), reads=[yt.r])

            nblk = NT // 512
            prev = None
            for blk in range(nblk + 1):
                cur = (blk, load_norm(blk)) if blk < nblk else None
                if prev is not None:
                    emit_out(*prev)
                prev = cur
            S.flush()

    def attn_stage(xin, xout):
        l = 1
        with contextlib.ExitStack() as st:
            wq = sb(st, "a_wq", [128, KD, 1024], BF16)
            wkd = sb(st, "a_wkd", [128, KD, 4, 2, 128], BF16)
            wv = sb(st, "a_wv", [128, KD, 256], BF16)
            wo = sb(st, "a_wo", [128, KD, 1024], BF16)
            biasT = sb(st, "a_bias", [128, 3, 16, 128], F32)
            xkv = Rot([sb(st, "a_x%d" % i, [128, KD, 768], F32) for i in range(1)])
            sq = sb(st, "a_sq", [128, KD, 512], BF16)
            lnv = sb(st, "a_ln", [128, 512], F32)
            tmpr = Rot([sb(st, "a_tmp%d" % i, [128, 512], F32) for i in range(2)])
            hT = sb(st, "a_hT", [128, KD, 768], BF16)
            QT = sb(st, "a_QT", [128, KD, 512], BF16)
            KT2 = sb(st, "a_KT", [128, 4, 2, 768], BF16)
            Vaug = sb(st, "a_V", [128, 6, 4, 66], BF16)
            tS = Rot([sb(st, "a_ts%d" % i, [128, 512], F32) for i in range(6)])
            pT = Rot([sb(st, "a_pT%d" % i, [128, 4, 128], BF16) for i in range(7)])
            den4 = Rot([sb(st, "a_den%d" % i, [128, 4], F32) for i in range(3)])
            r4 = Rot([sb(st, "a_r%d" % i, [128, 4], F32) for i in range(3)])
            AO = Rot([sb(st, "a_AO%d" % i, [128, D], BF16) for i in range(2)])
            AOT = sb(st, "a_AOT", [128, KD, 512], BF16)
            pproj = Rot([PB[0], PB[1]])
            pscore = Rot([PB[2], PB[3], PB[0], PB[1]])
            pacc = Rot([PB[4], PB[5]])
            pss = PB[6]
            ptr = PB[7]
            wsrc = wb_ain.h.ap().rearrange("(k p) n -> p k n", p=128)
            S.dma("sp", lambda e, s: e.dma_start(out=wq[:], in_=wsrc[:, :, 0:1024]).then_inc(s, 16), reads=[S.bgres[1]], writes=[wq.r])
            S.op("pool", lambda e: e.memset(wkd[:], 0.0), writes=[wkd.r])
            for g in range(4):
                for dup in range(2):
                    S.dma("sp", lambda e, s, g=g, dup=dup: e.dma_start(out=wkd[:, :, g, dup, dup * 64:(dup + 1) * 64],
                                                                       in_=wsrc[:, :, 1024 + g * 64:1024 + (g + 1) * 64]).then_inc(s, 16),
                          reads=[wkd.r, S.bgres[1]], writes=[wkd.r])
            S.dma("sp", lambda e, s: e.dma_start(out=wv[:], in_=wsrc[:, :, 1280:1536]).then_inc(s, 16), reads=[S.bgres[1]], writes=[wv.r])
            S.dma("sp", lambda e, s: e.dma_start(out=wo[:], in_=wb_aout.h.ap().rearrange("(k p) n -> p k n", p=128)).then_inc(s, 16), reads=[S.bgres[1]], writes=[wo.r])
            antiI = sb(st, "a_J", [128, 128], F32)
            hank = Rot([sb(st, "a_hk%d" % i, [128, 16, 128], F32) for i in range(1)])
            S.dma("sp", lambda e, s: e.dma_start(out=antiI[:], in_=c_anti.ap()).then_inc(s, 16), writes=[antiI.r])
            for rb in range(3):
                hk = hank.next()
                S.dma("sp", lambda e, s, rb=rb, hk=hk: e.dma_start(out=hk[:], in_=bass.AP(fd.h, rb * 128, [[1, 128], [512, 16], [1, 128]])).then_inc(s, 16),
                      writes=[hk.r])
                for h4 in range(4):
                    pt = pproj.next()

                    def mmb(e, hk=hk, h4=h4, pt=pt):
                        ins = None
                        for j in range(4):
                            ins = e.matmul(pt[:, j * 128:(j + 1) * 128], hk[:, h4 * 4 + j, :], antiI[:], start=True, stop=True)
                        return ins
                    S.op("pe", mmb, reads=[hk.r, antiI.r], writes=[pt.r])
                    S.op("act", lambda e, rb=rb, h4=h4, pt=pt: e.activation(biasT[:, rb, h4 * 4:h4 * 4 + 4, :], pt[:].rearrange("p (h q) -> p h q", h=4), AF.Copy),
                         reads=[pt.r], writes=[biasT.r])
            S.op("pool", lambda e: e.memset(Vaug[:], 1.0), writes=[Vaug.r])
            CUT = 9.0
            for si, (s0, SL) in enumerate(zip(starts, seqs)):
                if CUT <= 1:
                    break
                gs_ap = lambda k, si=si: gsT[:, l, 0, k, si:si + 1]
                sh_ap = lambda k, si=si: modT[:, l, 0, k, si:si + 1]

                def blk(i, si=si, s0=s0, SL=SL, gs_ap=gs_ap, sh_ap=sh_ap):
                    p = s0 + 512 * i
                    kc0 = max(p - 128, s0)
                    kc1 = min(p + 640, s0 + SL)
                    nk = kc1 - kc0
                    nkb = nk // 128
                    qoff = p - kc0
                    xt = xkv.next()
                    S.dma("sp", lambda e, s: e.dma_start(out=xt[:, :, 0:nk], in_=xT_dram(xin, kc0, kc1)).then_inc(s, 16),
                          reads=[xin.r], writes=[xt.r])
                    for c in range(0, nk, 512):
                        n_ = min(512, nk - c)
                        norm_block((sq, lnv, tmpr, pss), lambda k, c=c, n_=n_: xt[:, k, c:c + n_], n_, gs_ap, sh_ap, hT, hT.r, [xt.r], out_cols0=c)
                    for ob in range(KD):
                        pt = pproj.next()

                        def mmq(e, ob=ob, pt=pt):
                            ins = None
                            for k in range(KD):
                                ins = e.matmul(pt[:, 0:512], wq[:, k, ob * 128:(ob + 1) * 128], hT[:, k, qoff:qoff + 512], start=(k == 0), stop=(k == KD - 1))
                            return ins
                        S.op("pe", mmq, reads=[wq.r, hT.r], writes=[pt.r])
                        S.op("act", lambda e, ob=ob, pt=pt: e.activation(QT[:, ob, :], pt[:, 0:512], AF.Copy, scale=0.125), reads=[pt.r], writes=[QT.r])
                    for g in range(4):
                      for hp in range(2):
                        for c in range(0, nk, 512):
                            n_ = min(512, nk - c)
                            pt = pproj.next()

                            def mmk(e, g=g, hp=hp, c=c, n_=n_, pt=pt):
                                ins = None
                                for k in range(KD):
                                    ins = e.matmul(pt[:, 0:n_], wkd[:, k, g, hp, :], hT[:, k, c:c + n_], start=(k == 0), stop=(k == KD - 1))
                                return ins
                            S.op("pe", mmk, reads=[wkd.r, hT.r], writes=[pt.r])
                            S.op("dve", lambda e, g=g, hp=hp, c=c, n_=n_, pt=pt: e.tensor_copy(KT2[:, g, hp, c:c + n_], pt[:, 0:n_]), reads=[pt.r], writes=[KT2.r])
                    for kb in range(nkb):
                        pt = pproj.next()

                        def mmv(e, kb=kb, pt=pt):
                            ins = None
                            for k in range(KD):
                                ins = e.matmul(pt[:, 0:256], hT[:, k, kb * 128:(kb + 1) * 128], wv[:, k, :], start=(k == 0), stop=(k == KD - 1))
                            return ins
                        S.op("pe", mmv, reads=[wv.r, hT.r], writes=[pt.r])
                        S.op("act" if kb % 2 == 0 else "dve",
                             (lambda e, kb=kb, pt=pt: e.activation(Vaug[:, kb, :, 0:64], pt[:, 0:256].rearrange("p (g d) -> p g d", g=4), AF.Copy)) if kb % 2 == 0 else
                             (lambda e, kb=kb, pt=pt: e.tensor_copy(Vaug[:, kb, :, 0:64], pt[:, 0:256].rearrange("p (g d) -> p g d", g=4))),
                             reads=[pt.r], writes=[Vaug.r])
                    if CUT <= 2:
                        return
                    LAG = 4
                    pend = []

                    class _Q:
                        def append_m(self, f):
                            pend.append(("m", f))

                        def append_f(self, f):
                            pend.append(("f", f))
                    pend_mmo = pend_fin = None

                    def run_pending(keep=0):
                        while pend and (sum(1 for k_, _ in pend if k_ == "m") > keep or pend[0][0] == "f"):
                            pend.pop(0)[1]()

                    for qb in range(4):
                        kb_lo = qoff // 128 + qb - 1
                        valid = [rb for rb in range(3) if 0 <= kb_lo + rb < nkb]
                        ao = AO.next()
                        for g in range(4):
                            po = pacc.next()
                            for idx, rb in enumerate(valid):
                                kbl = kb_lo + rb
                                ps = pscore.next()

                                def mms(e, g=g, kbl=kbl, ps=ps, qb=qb):
                                    ins = None
                                    for i4 in range(4):
                                        h = 4 * g + i4
                                        ob, hp = h // 2, h % 2
                                        ins = e.matmul(ps[:, i4 * 128:(i4 + 1) * 128], KT2[:, g, hp, kbl * 128:(kbl + 1) * 128],
                                                       QT[:, ob, qb * 128:(qb + 1) * 128], start=True, stop=True)
                                    return ins
                                S.op("pe", mms, reads=[KT2.r, QT.r], writes=[ps.r])
                                ts = tS.next()
                                S.op("dve", lambda e, g=g, rb=rb, ps=ps, ts=ts: e.tensor_tensor(
                                    ts[:], ps[:], biasT[:, rb, 4 * g:4 * g + 4, :].rearrange("p h q -> p (h q)"), ALU.add),
                                    reads=[ps.r, biasT.r], writes=[ts.r])
                                pt_ = pT.next()
                                S.op("act", lambda e, ts=ts, pt_=pt_: e.activation(pt_[:].rearrange("p h q -> p (h q)"), ts[:], AF.Exp), reads=[ts.r], writes=[pt_.r])
                                run_pending(LAG - 1)

                                def mmo(e, g=g, kbl=kbl, po=po, pt_=pt_, idx=idx, last=(idx == len(valid) - 1)):
                                    ins = None
                                    for i4 in range(4):
                                        ins = e.matmul(po[:, i4 * 65:i4 * 65 + 65], pt_[:, i4, :], Vaug[:, kbl, g, 0:65],
                                                       start=(idx == 0 and i4 == 0), stop=last, skip_group_check=True)
                                    return ins
                                pend.append(("m", lambda mmo=mmo, pt_=pt_, po=po: S.op("pe", mmo, reads=[pt_.r, Vaug.r], writes=[po.r])))

                            def fin(g=g, po=po, ao=ao):
                                dn, rr = den4.next(), r4.next()
                                pov = po[:, 0:260].rearrange("p (h d) -> p h d", h=4)
                                S.op("dve", lambda e: e.tensor_tensor(dn[:], pov[:, :, 64], esink[:, 4 * g:4 * g + 4], ALU.add),
                                     reads=[po.r, esink.r], writes=[dn.r])
                                S.op("dve", lambda e: e.reciprocal(rr[:], dn[:]), reads=[dn.r], writes=[rr.r])
                                S.op("dve", lambda e: e.tensor_tensor(
                                    ao[:, g * 256:(g + 1) * 256].rearrange("p (h d) -> p h d", h=4), pov[:, :, 0:64],
                                    rr[:].unsqueeze(2).broadcast_to([128, 4, 64]), ALU.mult), reads=[po.r, rr.r], writes=[ao.r])
                            pend.append(("f", fin))

                        def trq(qb=qb, ao=ao):
                            ptb = ptr[:].bitcast(BF16)

                            def trp(e):
                                ins = None
                                for k in range(KD):
                                    ins = e.transpose(ptb[:, k * 128:(k + 1) * 128], ao[:, k * 128:(k + 1) * 128], identB[:])
                                return ins
                            S.op("pe", trp, reads=[ao.r, identB.r], writes=[ptr.r])
                            S.op("act", lambda e: e.activation(AOT[:, :, qb * 128:(qb + 1) * 128], ptb[:, 0:1024].rearrange("p (k t) -> p k t", k=KD), AF.Copy),
                                 reads=[ptr.r], writes=[AOT.r])
                        pend.append(("f", trq))
                    run_pending(0)
                    if CUT <= 3:
                        return
                    for db in range(KD):
                        pt = pproj.next()

                        def mmp(e, db=db, pt=pt):
                            ins = None
                            for k in range(KD):
                                ins = e.matmul(pt[:, 0:512], wo[:, k, db * 128:(db + 1) * 128], AOT[:, k, :], start=(k == 0), stop=(k == KD - 1))
                            return ins
                        S.op("pe", mmp, reads=[wo.r, AOT.r], writes=[pt.r])
                        S.op("dve", lambda e, db=db, pt=pt: e.scalar_tensor_tensor(
                            xt[:, db, qoff:qoff + 512], pt[:, 0:512], modT[:, l, 2, db, si:si + 1], xt[:, db, qoff:qoff + 512], ALU.mult, ALU.add),
                            reads=[pt.r, xt.r, modT.r], writes=[xt.r])
                    S.dma("sp", lambda e, s: e.dma_start(out=xT_dram(xout, p, p + 512), in_=xt[:, :, qoff:qoff + 512]).then_inc(s, 16),
                          reads=[xt.r], writes=[xout.r])
                for i in range(SL // 512):
                    blk(i)
            S.flush()

    def mlstm_pass(d):
        l = 0
        TB = 256
        with contextlib.ExitStack() as st:
            w = sb(st, "m_w", [128, KD, 3088], BF16)
            wsrc = wb_min.h.ap().rearrange("(k p) n -> p k n", p=128)
            if d == 1:
                S.dma("sp", lambda e, s: e.dma_start(out=w[:, :, 0:3072], in_=wsrc[:, :, 0:3072]).then_inc(s, 16), reads=[S.bgres[3]], writes=[w.r])
                S.dma("sp", lambda e, s: e.dma_start(out=w[:, :, 3072:3088], in_=wsrc[:, :, 4096:4112]).then_inc(s, 16), reads=[w.r, S.bgres[3]], writes=[w.r])
            else:
                stg = Rot([sb(st, "m_stg%d" % i, [128, KD, 512], F32) for i in range(2)])
                fsrc = m_w_in.ap().rearrange("(k p) n -> p k n", p=128)
                wparts = [Res() for _ in range(7)]
                for ci in range(7):
                    sg_ = stg.next()
                    c0, c1, o0 = (ci * 512, ci * 512 + 512, ci * 512) if ci < 6 else (4096, 4112, 3072)
                    S.dma("sp", lambda e, s, sg_=sg_, c0=c0, c1=c1: e.dma_start(out=sg_[:, :, 0:c1 - c0], in_=fsrc[:, :, c0:c1]).then_inc(s, 16), writes=[sg_.r])
                    if ci % 2 == 0:
                        S.op("act", lambda e, sg_=sg_, c0=c0, c1=c1, o0=o0: e.activation(w[:, :, o0:o0 + c1 - c0], sg_[:, :, 0:c1 - c0], AF.Copy), reads=[sg_.r], writes=[w.r])
                    else:
                        S.op("dve", lambda e, sg_=sg_, c0=c0, c1=c1, o0=o0: e.tensor_copy(w[:, :, o0:o0 + c1 - c0], sg_[:, :, 0:c1 - c0]), reads=[sg_.r], writes=[w.r])
            if d == 1:
                wo_in = sb(st, "m_woi", [128, KD, 1024], BF16)
                wout = sb(st, "m_wout", [128, KD, 1024], BF16)
                S.dma("sp", lambda e, s: e.dma_start(out=wo_in[:], in_=wsrc[:, :, 3072:4096]).then_inc(s, 16), reads=[S.bgres[3]], writes=[wo_in.r])
                S.dma("sp", lambda e, s: e.dma_start(out=wout[:], in_=wb_mout.h.ap().rearrange("(k p) n -> p k n", p=128)).then_inc(s, 16), reads=[S.bgres[3]], writes=[wout.r])
            xTr = Rot([sb(st, "m_xT%d" % i, [128, KD, TB], F32) for i in range(3 if d == 1 else 2)])
            sq = sb(st, "m_sq", [128, KD, TB], BF16)
            lnv = sb(st, "m_ln", [128, TB], F32)
            tmpr = Rot([sb(st, "m_tmp%d" % i, [128, TB], F32) for i in range(2 if d == 0 else 1)])
            hTr = Rot([sb(st, "m_hT%d" % i, [128, KD, TB], BF16) for i in range(2)])
            qTr = Rot([sb(st, "m_qT%d" % i, [128, KD, TB], BF16) for i in range(2)])
            kTr = Rot([sb(st, "m_kT%d" % i, [128, KD, TB], BF16) for i in range(2)])
            kpr = Rot([sb(st, "m_kp%d" % i, [128, 1024], BF16) for i in range(4)])
            kunr = Rot([sb(st, "m_kun%d" % i, [128, 1024], BF16) for i in range(1)])
            var = Rot([sb(st, "m_va%d" % i, [128, 4, 256], BF16) for i in range(4)])
            onec = sb(st, "m_onec", [128, 2], BF16)
            S.op("dve", lambda e: e.memset(onec[:], 1.0), writes=[onec.r])
            gtr = Rot([sb(st, "m_gt%d" % i, [128, 16], F32) for i in range(4)])
            smr = Rot([sb(st, "m_sm%d" % i, [128, 8, 4], F32) for i in range(4)])
            t4r = Rot([sb(st, "m_t4%d" % i, [128, 2, 4], F32) for i in range(2)])
            hfr = Rot([sb(st, "m_hf%d" % i, [128, D], F32) for i in range(2 if d == 0 else 1)])
            C32 = sb(st, "m_C32", [128, 4, 2, 257], F32)
            Cb = sb(st, "m_Cb", [128, 4, 2, 258], BF16)
            Sdr = Rot([sb(st, "m_Sd%d" % i, [128, 4, 128], BF16) for i in range(2 if d == 0 else 1)])
            if d == 1:
                hsr = Rot([sb(st, "m_hs%d" % i, [128, D], F32) for i in range(2)])
                ogr = Rot([sb(st, "m_og%d" % i, [128, D], F32) for i in range(1)])
                gar = Rot([sb(st, "m_ga%d" % i, [128, D], BF16) for i in range(2)])
                gTr = Rot([sb(st, "m_gT%d" % i, [128, KD, TB], BF16) for i in range(2)])
                junk = sb(st, "m_junk", [128, 256], BF16)
                ssr = Rot([sb(st, "m_ss%d" % i, [128, 4], F32) for i in range(2)])
            pone = Rot([PB[0], PB[1]])
            pfs = PB[2]
            pbs = PB[3]
            pscore = PB[4]
            pnum = (PB[5], PB[6])
            pcu = PB[7]

            def front(si, s0, p, fst):
                gs_ap = lambda k: gsT[:, l, 0, k, si:si + 1]
                sh_ap = lambda k: modT[:, l, 0, k, si:si + 1]
                xt, hT, qT, kT = xTr.next(), hTr.next(), qTr.next(), kTr.next()
                fst.update(xt=xt, hT=hT, qT=qT, kT=kT, ch={})
                S.dma("sp", lambda e, s: e.dma_start(out=xt[:], in_=xT_dram(xa, p, p + TB)).then_inc(s, 16), reads=[xa.r], writes=[xt.r])
                norm_block((sq, lnv, tmpr, pone.next()), lambda k: xt[:, k, :], TB, gs_ap, sh_ap, hT, hT.r, [xt.r])
                yield
                chs = (0, 1) if d == 0 else (1, 0)
                for slot, ch in enumerate(chs):
                    cs = ch * 128
                    gt, sm = gtr.next(), smr.next()
                    fst["ch"][ch] = dict(gt=gt, sm=sm, kp=kpr.next(), va=var.next())
                    g0 = 64 * slot

                    def mmg(e, cs=cs, g0=g0):
                        ins = None
                        for k in range(KD):
                            ins = e.matmul(pfs[:, g0:g0 + 16], hT[:, k, cs:cs + 128], w[:, k, 3072:3088], start=(k == 0), stop=(k == KD - 1))
                        return ins
                    S.op("pe", mmg, reads=[w.r, hT.r], writes=[pfs.r])
                    S.op("dve", lambda e, gt=gt, g0=g0: e.tensor_tensor(gt[:], pfs[:, g0:g0 + 16], bcT[:, 0:16], ALU.add), reads=[pfs.r, bcT.r], writes=[gt.r])
                    S.op("act", lambda e, gt=gt, sm=sm: e.activation(sm[:, 0, :], gt[:, d * 8 + 4:d * 8 + 8], AF.Exp, scale=-1.0), reads=[gt.r], writes=[sm.r])
                    S.op("act", lambda e, sm=sm: e.activation(sm[:, 1, :], sm[:, 0, :], AF.Ln, bias=1.0), reads=[sm.r], writes=[sm.r])
                yield
                for which, dst, c0 in ((0, qT, 0),):
                    for ob in range(KD):
                        pt = pone.next()

                        def mmf(e, ob=ob, pt=pt, c0=c0):
                            ins = None
                            for k in range(KD):
                                ins = e.matmul(pt[:, 0:TB], w[:, k, c0 + ob * 128:c0 + (ob + 1) * 128], hT[:, k, :], start=(k == 0), stop=(k == KD - 1))
                            return ins
                        S.op("pe", mmf, reads=[w.r, hT.r], writes=[pt.r])
                        if which == 0:
                            S.op("act", lambda e, ob=ob, pt=pt: e.activation(qT[:, ob, :], pt[:, 0:TB], AF.Copy, scale=1.0 / 16.0), reads=[pt.r], writes=[qT.r])
                        else:
                            S.op("dve", lambda e, ob=ob, pt=pt: e.tensor_copy(kT[:, ob, :], pt[:, 0:TB]), reads=[pt.r], writes=[kT.r])
                        if ob % 2 == 1:
                            yield
                    if which == 0:
                        for slot, ch in enumerate(chs):
                            c = fst["ch"][ch]
                            gt, sm = c["gt"], c["sm"]
                            g0 = 64 * slot

                            def mmc(e, sm=sm, g0=g0):
                                e.matmul(pfs[:, g0 + 16:g0 + 20], triF[:, d, :], sm[:, 1, :], start=True, stop=True)
                                return e.matmul(pfs[:, g0 + 20:g0 + 24], onesF[:], sm[:, 1, :], start=True, stop=True)
                            S.op("pe", mmc, reads=[sm.r, triF.r, onesF.r], writes=[pfs.r])
                            S.op("act", lambda e, sm=sm, g0=g0: e.activation(sm[:, 0:2, :], pfs[:, g0 + 16:g0 + 24].rearrange("p (a b) -> p a b", a=2), AF.Copy),
                                 reads=[pfs.r, sm.r], writes=[sm.r])
                            S.op("dve", lambda e, gt=gt, sm=sm: e.tensor_tensor(sm[:, 2, :], sm[:, 0, :], gt[:, d * 8:d * 8 + 4], ALU.add),
                                 reads=[sm.r, gt.r], writes=[sm.r])
                            S.op("dve", lambda e, sm=sm: e.tensor_tensor(sm[:, 3, :], sm[:, 2, :], sm[:, 1, :], ALU.subtract),
                                 reads=[sm.r], writes=[sm.r])
                            S.op("act", lambda e, sm=sm: e.activation(sm[:, 6:8, :], sm[:, 0:2, :], AF.Exp, scale=-1.0), reads=[sm.r], writes=[sm.r])
                            S.op("act", lambda e, sm=sm: e.activation(sm[:, 0, :], sm[:, 0, :], AF.Exp), reads=[sm.r], writes=[sm.r])
                            S.op("act", lambda e, sm=sm: e.activation(sm[:, 4, :], sm[:, 2, :], AF.Exp), reads=[sm.r], writes=[sm.r])
                            S.op("act", lambda e, sm=sm: e.activation(sm[:, 5, :], sm[:, 3, :], AF.Exp), reads=[sm.r], writes=[sm.r])
                        yield
                for ch in chs:
                    cs = ch * 128
                    c = fst["ch"][ch]
                    sm, kp, va = c["sm"], c["kp"], c["va"]
                    for half in range(2):
                        pt = pone.next()

                        def mmk(e, pt=pt, half=half, cs=cs):
                            ins = None
                            for k in range(KD):
                                ins = e.matmul(pt[:, :], hT[:, k, cs:cs + 128], w[:, k, 1024 + half * 512:1024 + (half + 1) * 512], start=(k == 0), stop=(k == KD - 1))
                            return ins
                        S.op("pe", mmk, reads=[w.r, hT.r], writes=[pt.r])
                        for j in range(2):
                            h = half * 2 + j
                            if j == 0:
                                S.op("act", lambda e, h=h, pt=pt, kp=kp, sm=sm: e.activation(kp[:, h * 256:(h + 1) * 256], pt[:, 0:256], AF.Copy, scale=sm[:, 5, h:h + 1]),
                                     reads=[pt.r, sm.r], writes=[kp.r])
                            else:
                                S.op("dve", lambda e, h=h, pt=pt, kp=kp, sm=sm: e.tensor_scalar(kp[:, h * 256:(h + 1) * 256], pt[:, 256:512], sm[:, 5, h:h + 1], None, ALU.mult),
                                     reads=[pt.r, sm.r], writes=[kp.r])
                        if half == 0:
                            kun = kunr.next()
                        S.op("act", lambda e, pt=pt, kun=kun, half=half: e.activation(kun[:, half * 512:half * 512 + 256], pt[:, 0:256], AF.Copy), reads=[pt.r], writes=[kun.r])
                        S.op("dve", lambda e, pt=pt, kun=kun, half=half: e.tensor_copy(kun[:, half * 512 + 256:half * 512 + 512], pt[:, 256:512]), reads=[pt.r], writes=[kun.r])
                        yield
                    ptk = pone.next()
                    ptkb = ptk[:].bitcast(BF16)

                    def trk(e, kun=kun, ptkb=ptkb):
                        ins = None
                        for ob in range(KD):
                            ins = e.transpose(ptkb[:, ob * 128:(ob + 1) * 128], kun[:, ob * 128:(ob + 1) * 128], identB[:])
                        return ins
                    S.op("pe", trk, reads=[kun.r, identB.r], writes=[ptk.r])
                    S.op("act", lambda e, cs=cs, ptkb=ptkb: e.activation(kT[:, :, cs:cs + 128], ptkb[:, 0:1024].rearrange("p (k t) -> p k t", k=KD), AF.Copy),
                         reads=[ptk.r], writes=[kT.r])
                    yield
                    for half in range(2):
                        pt = pone.next()

                        def mmv(e, pt=pt, half=half, cs=cs):
                            ins = None
                            for k in range(KD):
                                ins = e.matmul(pt[:, :], hT[:, k, cs:cs + 128], w[:, k, 2048 + half * 512:2048 + (half + 1) * 512], start=(k == 0), stop=(k == KD - 1))
                            return ins
                        S.op("pe", mmv, reads=[w.r, hT.r], writes=[pt.r])
                        if half == 0:
                            S.op("act", lambda e, pt=pt, va=va: e.activation(va[:, 0:2, :], pt[:, :].rearrange("p (h c) -> p h c", h=2), AF.Copy), reads=[pt.r], writes=[va.r])
                        else:
                            S.op("dve", lambda e, pt=pt, va=va: e.tensor_copy(va[:, 2:4, :], pt[:, :].rearrange("p (h c) -> p h c", h=2)), reads=[pt.r], writes=[va.r])
                        yield

            def back(si, p, fst):
                xt, hT, qT, kT = fst["xt"], fst["hT"], fst["qT"], fst["kT"]
                chs = (0, 1) if d == 0 else (1, 0)
                gT = gTr.next() if d == 1 else None
                for ch in chs:
                    cs = ch * 128
                    tok = p + cs
                    c = fst["ch"][ch]
                    sm, kp, va = c["sm"], c["kp"], c["va"]
                    if d == 1:
                        hfl = hfr.next()
                        S.dma("sp", lambda e, s, hfl=hfl, tok=tok: e.dma_start(out=hfl[:], in_=hfw.h.ap()[tok:tok + 128, :]).then_inc(s, 16), reads=[hfw.r], writes=[hfl.r])

                    def mms(e, cs=cs):
                        ins = None
                        for h in range(4):
                            for kb in range(2):
                                ins = e.matmul(pscore[:, h * 128:(h + 1) * 128], kT[:, 2 * h + kb, cs:cs + 128], qT[:, 2 * h + kb, cs:cs + 128], start=(kb == 0), stop=(kb == 1))
                        return ins
                    S.op("pe", mms, reads=[kT.r, qT.r], writes=[pscore.r])
                    Sd = Sdr.next()
                    for h in range(4):
                        S.op("dve", lambda e, h=h, Sd=Sd, sm=sm: e.scalar_tensor_tensor(Sd[:, h, :], pscore[:, h * 128:(h + 1) * 128], sm[:, 4, h:h + 1], triF[:, d, :], ALU.mult, ALU.mult),
                             reads=[pscore.r, sm.r, triF.r], writes=[Sd.r])
                    nold = fmark[0]
                    for _ in range(nold):
                        deferred.pop(0)()
                    fmark[0] = len(deferred)
                    yield

                    def mmn(e, cs=cs, Sd=Sd, va=va):
                        ins = None
                        for h in range(4):
                            pn = pnum[h // 2][:, (h % 2) * 256:(h % 2) * 256 + 256]
                            e.matmul(pn, Sd[:, h, :], va[:, h, :], start=True, stop=False)
                            e.matmul(pn, qT[:, 2 * h, cs:cs + 128], Cb[:, h, 0, 0:256], start=False, stop=False)
                            ins = e.matmul(pn, qT[:, 2 * h + 1, cs:cs + 128], Cb[:, h, 1, 0:256], start=False, stop=True)
                        return ins
                    S.op("pe", mmn, reads=[Sd.r, va.r, qT.r, Cb.r], writes=[pnum[0].r, pnum[1].r])

                    def mmd(e, cs=cs, Sd=Sd):
                        ins = None
                        for h in range(4):
                            e.matmul(pbs[:, h:h + 1], Sd[:, h, :], onec[:, 0:1], start=True, stop=False)
                            e.matmul(pbs[:, h:h + 1], qT[:, 2 * h, cs:cs + 128], Cb[:, h, 0, 256:257], start=False, stop=False)
                            ins = e.matmul(pbs[:, h:h + 1], qT[:, 2 * h + 1, cs:cs + 128], Cb[:, h, 1, 256:257], start=False, stop=True)
                        return ins
                    S.op("pe", mmd, reads=[Sd.r, onec.r, qT.r, Cb.r], writes=[pbs.r])
                    t4 = t4r.next()
                    EB = sm[:, 6, :]
                    S.op("act", lambda e, t4=t4: e.activation(t4[:, 0, :], pbs[:, 0:4], AF.Abs), reads=[pbs.r], writes=[t4.r])
                    S.op("dve", lambda e, t4=t4, sm=sm: e.tensor_tensor(t4[:, 0, :], t4[:, 0, :], sm[:, 0, :], ALU.max), reads=[t4.r, sm.r], writes=[t4.r])
                    S.op("dve", lambda e, t4=t4: e.reciprocal(t4[:, 1, :], t4[:, 0, :]), reads=[t4.r], writes=[t4.r])
                    hout = hfr.next() if d == 0 else hsr.next()
                    for h in range(4):
                        pn = pnum[h // 2][:, (h % 2) * 256:(h % 2) * 256 + 256]
                        if d == 0:
                            if h % 2 == 0:
                                S.op("act", lambda e, h=h, pn=pn, t4=t4, hout=hout: e.activation(hout[:, h * 256:(h + 1) * 256], pn, AF.Copy, scale=t4[:, 1, h:h + 1]),
                                     reads=[pnum[h // 2].r, t4.r], writes=[hout.r])
                            else:
                                S.op("dve", lambda e, h=h, pn=pn, t4=t4, hout=hout: e.tensor_scalar(hout[:, h * 256:(h + 1) * 256], pn, t4[:, 1, h:h + 1], None, ALU.mult),
                                     reads=[pnum[h // 2].r, t4.r], writes=[hout.r])
                        else:
                            S.op("dve", lambda e, h=h, pn=pn, t4=t4, hout=hout, hfl=hfl: e.scalar_tensor_tensor(
                                hout[:, h * 256:(h + 1) * 256], pn, t4[:, 1, h:h + 1], hfl[:, h * 256:(h + 1) * 256], ALU.mult, ALU.add),
                                reads=[pnum[h // 2].r, t4.r, hfl.r], writes=[hout.r])
                    yield
                    if d == 1:
                        og = ogr.next()
                        for half in range(2):
                            pt = pone.next()

                            def mmo(e, pt=pt, half=half, cs=cs):
                                ins = None
                                for k in range(KD):
                                    ins = e.matmul(pt[:, :], hT[:, k, cs:cs + 128], wo_in[:, k, half * 512:(half + 1) * 512], start=(k == 0), stop=(k == KD - 1))
                                return ins
                            S.op("pe", mmo, reads=[wo_in.r, hT.r], writes=[pt.r])
                            S.op("act", lambda e, half=half, pt=pt, og=og: e.activation(og[:, half * 512:(half + 1) * 512], pt[:, :], AF.Sigmoid), reads=[pt.r], writes=[og.r])
                        yield
                    for h in range(4):
                        pc = (pcu, pscore)[h % 2]

                        def mmu(e, h=h, kp=kp, va=va, pc=pc):
                            e.matmul(pc[:, 0:256], kp[:, h * 256:h * 256 + 128], va[:, h, :], start=True, stop=True)
                            return e.matmul(pc[:, 256:512], kp[:, h * 256 + 128:h * 256 + 256], va[:, h, :], start=True, stop=True)
                        S.op("pe", mmu, reads=[kp.r, va.r], writes=[pc.r])
                        S.op("dve", lambda e, h=h, sm=sm, pc=pc: e.scalar_tensor_tensor(C32[:, h, :, 0:256], C32[:, h, :, 0:256], sm[:, 7, h:h + 1],
                                                                                  pc[:, :].rearrange("p (kb c) -> p kb c", kb=2), ALU.mult, ALU.add),
                             reads=[pc.r, sm.r, C32.r], writes=[C32.r])

                    def mmnu(e, kp=kp):
                        ins = None
                        for h in range(4):
                            for kb in range(2):
                                ins = e.matmul(pbs[:, 8 + 2 * h + kb:9 + 2 * h + kb], kp[:, h * 256 + kb * 128:h * 256 + (kb + 1) * 128], onec[:, 0:1], start=True, stop=True)
                        return ins
                    S.op("pe", mmnu, reads=[kp.r, onec.r], writes=[pbs.r])
                    S.op("dve", lambda e, sm=sm: e.tensor_tensor(C32[:, :, :, 256], C32[:, :, :, 256], sm[:, 7, :].unsqueeze(2).broadcast_to([128, 4, 2]), ALU.mult),
                         reads=[sm.r, C32.r], writes=[C32.r])
                    S.op("dve", lambda e: e.tensor_tensor(C32[:, :, :, 256], C32[:, :, :, 256], pbs[:, 8:16].rearrange("p (h kb) -> p h kb", kb=2), ALU.add),
                         reads=[pbs.r, C32.r], writes=[C32.r])
                    S.op("act", lambda e: e.activation(Cb[:, :, :, 0:257], C32[:, :, :, :], AF.Copy), reads=[C32.r], writes=[Cb.r])
                    yield
                    if d == 0:
                        S.dma("sp", lambda e, s, hout=hout, tok=tok: e.dma_start(out=hfw.h.ap()[tok:tok + 128, :], in_=hout[:]).then_inc(s, 16), reads=[hout.r], writes=[hfw.r])
                        continue
                    ga, ss = gar.next(), ssr.next()
                    gm = hout
                    for h in range(4):
                        S.op("act", lambda e, h=h, hout=hout, ss=ss: e.activation(junk[:], hout[:, h * 256:(h + 1) * 256], AF.Square, accum_out=ss[:, h:h + 1]),
                             reads=[hout.r], writes=[junk.r, ss.r])
                    S.op("act", lambda e, ss=ss: e.activation(ss[:], ss[:], AF.Ln, bias=EPS, scale=1.0 / 256.0), reads=[ss.r], writes=[ss.r])
                    S.op("act", lambda e, ss=ss: e.activation(ss[:], ss[:], AF.Exp, scale=-0.5), reads=[ss.r], writes=[ss.r])
                    for h in range(4):
                        S.op("dve", lambda e, h=h, hout=hout, ss=ss, gm=gm: e.scalar_tensor_tensor(gm[:, h * 256:(h + 1) * 256], hout[:, h * 256:(h + 1) * 256], ss[:, h:h + 1],
                                                                                                  bcT[:, 16 + h * 256:16 + (h + 1) * 256], ALU.mult, ALU.mult),
                             reads=[hout.r, ss.r, bcT.r], writes=[gm.r])
                    S.op("pool", lambda e, ga=ga, gm=gm, og=og: e.tensor_tensor(ga[:], gm[:], og[:], ALU.mult), reads=[gm.r, og.r], writes=[ga.r])
                    ptb_t = pone.next()
                    ptb = ptb_t[:].bitcast(BF16)

                    def trp(e, ga=ga, ptb=ptb):
                        ins = None
                        for k in range(KD):
                            ins = e.transpose(ptb[:, k * 128:(k + 1) * 128], ga[:, k * 128:(k + 1) * 128], identB[:])
                        return ins
                    def ep_pe(trp=trp, ga=ga, ptb_t=ptb_t, ptb=ptb, cs=cs, gT=gT):
                        S.op("pe", trp, reads=[ga.r, identB.r], writes=[ptb_t.r])
                        S.op("act", lambda e: e.activation(gT[:, :, cs:cs + 128], ptb[:, 0:1024].rearrange("p (k t) -> p k t", k=KD), AF.Copy),
                             reads=[ptb_t.r], writes=[gT.r])
                    deferred.append(ep_pe)
                    yield
                if d == 1:
                    deferred.append(lambda: wout_part(si, p, xt, gT))

            def wout_part(si, p, xt, gT):
                if True:
                    for db in range(KD):
                        pt = pone.next()

                        def mmp(e, db=db, pt=pt, gT=gT):
                            ins = None
                            for k in range(KD):
                                ins = e.matmul(pt[:, 0:TB], wout[:, k, db * 128:(db + 1) * 128], gT[:, k, :], start=(k == 0), stop=(k == KD - 1))
                            return ins
                        S.op("pe", mmp, reads=[wout.r, gT.r], writes=[pt.r])
                        S.op("dve", lambda e, db=db, pt=pt, xt=xt: e.scalar_tensor_tensor(
                            xt[:, db, :], pt[:, 0:TB], modT[:, l, 2, db, si:si + 1], xt[:, db, :], ALU.mult, ALU.add),
                            reads=[pt.r, xt.r, modT.r], writes=[xt.r])
                    S.dma("sp", lambda e, s, xt=xt: e.dma_start(out=xT_dram(xb, p, p + TB), in_=xt[:]).then_inc(s, 16), reads=[xt.r], writes=[xb.r])

            deferred = []
            fmark = [0]

            def drain(g):
                for _ in g:
                    pass

            def interleave(ga_, gb_, ratio):
                a_done = b_done = False
                while not (a_done and b_done):
                    if not a_done:
                        try:
                            next(ga_)
                        except StopIteration:
                            a_done = True
                    for _ in range(ratio):
                        if b_done:
                            break
                        try:
                            next(gb_)
                        except StopIteration:
                            b_done = True

            for si, (s0, SL) in enumerate(zip(starts, seqs)):
                S.op("dve", lambda e: e.memset(C32[:], 0.0), writes=[C32.r])
                S.op("dve", lambda e: e.memset(Cb[:], 0.0), writes=[Cb.r])
                nblk = SL // TB
                order = list(range(nblk)) if d == 0 else list(range(nblk - 1, -1, -1))
                fsts = [dict() for _ in order]
                drain(front(si, s0, s0 + TB * order[0], fsts[0]))
                for j, bi in enumerate(order):
                    bg = back(si, s0 + TB * bi, fsts[j])
                    if j + 1 < len(order):
                        fg = front(si, s0, s0 + TB * order[j + 1], fsts[j + 1])
                        interleave(bg, fg, 3)
                    else:
                        drain(bg)
                    drip(3, C32.r)
                while deferred:
                    deferred.pop(0)()
                fmark[0] = 0
            if d == 1:
                drip(len(castq), C32.r)
            S.flush()

    ctx = dict(final_stage=final_stage, ffn_stage=ffn_stage, xpose_stage=(lambda: None), attn_stage=attn_stage, mlstm_pass=mlstm_pass, xa=xa, xb=xb, hfw=hfw)
    return nc, S, top, ctx


def host_inputs(seq_arrays, c_rows, w):
    NS = len(seq_arrays)
    m = {}
    m["x"] = np.ascontiguousarray(np.concatenate(seq_arrays, axis=0), dtype=np.float32)
    vecs = np.zeros((512, 128), np.float32)
    vecs[0:96] = np.asarray(w["adaln_b"], np.float32).reshape(96, 128)
    vecs[96:128] = np.asarray(w["norm_g"], np.float32).reshape(32, 128)
    vecs[128:136] = np.asarray(w["final_g"], np.float32).reshape(8, 128)
    cc = np.stack([np.asarray(c, np.float32).reshape(8, 128) for c in c_rows], axis=1)
    vecs[136:136 + 8 * NS] = cc.reshape(8 * NS, 128)
    vecs[152:416] = np.asarray(w["ffn_conv_w"], np.float32).reshape(2 * 3 * 44, 128)
    vecs[416:504] = np.asarray(w["ffn_conv_b"], np.float32).reshape(2 * 44, 128)
    m["vecs"] = vecs
    bc = np.concatenate([np.asarray(w["mlstm_b_gate"], np.float32).reshape(16),
                         np.asarray(w["mlstm_head_g"], np.float32).reshape(1024),
                         np.asarray(w["attn_sink"], np.float32).reshape(16)])
    m["bc"] = np.ascontiguousarray(np.broadcast_to(bc[None, :], (128, bc.size)))
    m["adaln_w"] = np.asarray(w["adaln_w"], np.float32)
    m["mlstm_w_in"] = np.asarray(w["mlstm_w_in"], np.float32)[0]
    m["mlstm_w_out"] = np.asarray(w["mlstm_w_out"], np.float32)[0]
    m["attn_w_in"] = np.asarray(w["attn_w_in"], np.float32)[0]
    m["attn_w_out"] = np.asarray(w["attn_w_out"], np.float32)[0]
    m["rel_bias"] = np.asarray(w["rel_bias"], np.float32)
    m["ffn_w_up"] = np.asarray(w["ffn_w_up"], np.float32)
    m["ffn_w_down"] = np.asarray(w["ffn_w_down"], np.float32)
    m.update(_consts())
    return m


def emit_all(ctx):
    ctx["xpose_stage"]()
    ctx["mlstm_pass"](0)
    ctx["mlstm_pass"](1)
    ctx["ffn_stage"](0, ctx["xb"], ctx["xa"], False)
    ctx["attn_stage"](ctx["xa"], ctx["xb"])
    ctx["ffn_stage"](1, ctx["xb"], ctx["xa"], False)
    ctx["final_stage"](ctx["xa"])


_PROG = {}


def _program(seqs):
    key = tuple(seqs)
    if key not in _PROG:
        nc, S, top, ctx = build(list(seqs))
        emit_all(ctx)
        top.close()
        _PROG[key] = nc
    return _PROG[key]


def kernel(x_prompt, x_sample, c_prompt, c_sample, adaln_w, adaln_b, norm_g, mlstm_w_in, mlstm_b_gate,
           mlstm_head_g, mlstm_w_out, attn_w_in, attn_sink, attn_w_out, rel_bias, ffn_w_up, ffn_conv_w,
           ffn_conv_b, ffn_w_down, final_g):
    x_prompt = np.asarray(x_prompt, np.float32)
    x_sample = np.asarray(x_sample, np.float32)
    c_prompt = np.asarray(c_prompt, np.float32)
    c_sample = np.asarray(c_sample, np.float32)
    n = 8
    SP, SS = x_prompt.shape[1], x_sample.shape[1]
    w = dict(adaln_w=adaln_w, adaln_b=adaln_b, norm_g=norm_g, mlstm_w_in=mlstm_w_in, mlstm_b_gate=mlstm_b_gate,
             mlstm_head_g=mlstm_head_g, mlstm_w_out=mlstm_w_out, attn_w_in=attn_w_in, attn_sink=attn_sink,
             attn_w_out=attn_w_out, rel_bias=rel_bias, ffn_w_up=ffn_w_up, ffn_conv_w=ffn_conv_w,
             ffn_conv_b=ffn_conv_b, ffn_w_down=ffn_w_down, final_g=final_g)
    nc = _program([SP, SS])
    base = host_inputs([x_prompt[0], x_sample[0]], [c_prompt[0], c_sample[0]], w)
    in_maps = []
    for i in range(n):
        m = dict(base)
        if i > 0:
            per = host_inputs([x_prompt[i], x_sample[i]], [c_prompt[i], c_sample[i]], w)
            m["x"] = per["x"]
            m["vecs"] = per["vecs"]
        in_maps.append(m)
    res = run_bass_kernel_spmd(nc, in_maps, core_ids=list(range(n)))
    ys = [np.asarray(r["y"], np.float32) for r in res.results]
    y_prompt = np.stack([y[:SP] for y in ys], axis=0)
    y_sample = np.stack([y[SP:SP + SS] for y in ys], axis=0)
    return (y_prompt, y_sample)
```
